# Optimizing a Trainium2 kernel written in Bass

```python
import math
import jax
import jax.numpy as jnp
from jax import lax
import numpy as np

D_MODEL = 1024
BATCH = 4
SEQ = 4096
DEPTH = 4
DEC_BATCH = 16
DEC_SEQ = 32
PAST_LEN = 1024

CHUNK = 64
D_A = D_MODEL
H_A = 8
DK_A = D_A // H_A
DV_A = D_A // H_A
SHORT_CONV = 4
C_B = D_MODEL
CONV_B = 31
H_C = 8
DH_C = 128
H_KV = 2
D_C = H_C * DH_C
H_I = 8
D_I = 64
TOPK = 256
ROPE_THETA = 500000.0
ROPE_FRACTION = 4
EPS = 1e-6
IN_SIZES = (3 * D_A, H_A, H_A, D_A, 2 * C_B, C_B, D_C, H_KV * DH_C, H_KV * DH_C, H_I * D_I, D_I, H_I, D_C, 3 * D_MODEL)
N_IN = 3 * D_A + 2 * H_A + D_A + 3 * C_B + 2 * D_C + 2 * H_KV * DH_C + H_I * D_I + D_I + H_I + 3 * D_MODEL

kernel_name = 'hybrid_stream_deltanet_conformer_dsa_step'


def rms_norm(x, g):
    x32 = x.astype(jnp.float32)
    y = x32 * lax.rsqrt(jnp.mean(x32 * x32, axis=-1, keepdims=True) + EPS)
    return (y * g.astype(jnp.float32)).astype(x.dtype)


def l2_norm(x):
    x32 = x.astype(jnp.float32)
    return (x32 * lax.rsqrt(jnp.sum(x32 * x32, axis=-1, keepdims=True) + EPS)).astype(x.dtype)


def layer_norm(x, g, b):
    x32 = x.astype(jnp.float32)
    mu = jnp.mean(x32, axis=-1, keepdims=True)
    xc = x32 - mu
    var = jnp.mean(xc * xc, axis=-1, keepdims=True)
    return (xc * lax.rsqrt(var + EPS) * g.astype(jnp.float32) + b.astype(jnp.float32)).astype(x.dtype)


def split_cols(t, sizes):
    offs = []
    acc = 0
    for s in sizes[:-1]:
        acc += s
        offs.append(acc)
    return jnp.split(t, offs, axis=-1)


def partial_rope(x, pos):
    d = x.shape[-1]
    rot = d // ROPE_FRACTION
    half = rot // 2
    inv = ROPE_THETA ** (-jnp.arange(half, dtype=jnp.float32) * 2.0 / rot)
    ang = pos.astype(jnp.float32)[:, None] * inv[None, :]
    cos = jnp.cos(ang)[None, :, None, :].astype(x.dtype)
    sin = jnp.sin(ang)[None, :, None, :].astype(x.dtype)
    x1, x2, xr = x[..., :half], x[..., half:rot], x[..., rot:]
    return jnp.concatenate([x1 * cos - x2 * sin, x2 * cos + x1 * sin, xr], axis=-1)


def causal_dwconv(x, hist, w):
    xp = jnp.concatenate([hist.astype(x.dtype), x], axis=1)
    c = x.shape[-1]
    y = lax.conv_general_dilated(xp, w[:, None, :].astype(x.dtype), window_strides=(1,), padding='VALID',
                                 dimension_numbers=('NWC', 'WIO', 'NWC'), feature_group_count=c)
    return y, xp[:, xp.shape[1] - (w.shape[0] - 1):]


def gated_delta_rule(q, k, v, g, beta, s0):
    bsz, t_len, nh, _ = q.shape
    dv = v.shape[-1]
    out_dtype = v.dtype
    c = CHUNK if t_len % CHUNK == 0 else t_len
    n = t_len // c
    f32 = jnp.float32

    def to_chunks(t):
        t = t.astype(f32).reshape((bsz, n, c, nh) + t.shape[3:])
        return jnp.moveaxis(t, 3, 1)

    q, k, v, g, beta = to_chunks(q), to_chunks(k), to_chunks(v), to_chunks(g), to_chunks(beta)
    big_g = jnp.cumsum(g, axis=-1)
    ar = jnp.arange(c)
    incl = ar[:, None] >= ar[None, :]
    strict = ar[:, None] > ar[None, :]
    diff = big_g[..., :, None] - big_g[..., None, :]
    dec_incl = jnp.exp(jnp.where(incl, diff, -jnp.inf))
    dec_strict = jnp.exp(jnp.where(strict, diff, -jnp.inf))
    kb = k * beta[..., None]
    a_mat = jnp.eye(c, dtype=f32) + jnp.einsum('bhnid,bhnjd->bhnij', kb, k) * dec_strict
    u = lax.linalg.triangular_solve(a_mat, v * beta[..., None], left_side=True, lower=True, unit_diagonal=True)
    w = lax.linalg.triangular_solve(a_mat, kb * jnp.exp(big_g)[..., None], left_side=True, lower=True, unit_diagonal=True)
    qk = jnp.einsum('bhnid,bhnjd->bhnij', q, k) * dec_incl
    q_dec = q * jnp.exp(big_g)[..., None]
    k_dec = k * jnp.exp(big_g[..., -1:] - big_g)[..., None]
    g_end = jnp.exp(big_g[..., -1])

    def step(s, xs):
        u_c, w_c, qk_c, qd_c, kd_c, ge_c = xs
        v_new = u_c - jnp.einsum('bhcd,bhde->bhce', w_c, s)
        o_c = jnp.einsum('bhcd,bhde->bhce', qd_c, s) + jnp.einsum('bhij,bhje->bhie', qk_c, v_new)
        s = ge_c[..., None, None] * s + jnp.einsum('bhcd,bhce->bhde', kd_c, v_new)
        return s, o_c

    xs = tuple(jnp.moveaxis(t, 2, 0) for t in (u, w, qk, q_dec, k_dec, g_end))
    s_fin, o = lax.scan(step, s0.astype(f32), xs)
    o = jnp.transpose(o, (1, 0, 3, 2, 4)).reshape(bsz, t_len, nh, dv)
    return o.astype(out_dtype), s_fin.astype(s0.dtype)


def dsa_attention(q, qi, wi, k, v, ki, q_pos):
    bsz, t_len = q.shape[0], q.shape[1]
    l_keys = k.shape[1]
    n_sel = min(TOPK, l_keys // 4)
    qb = 128 if t_len % 128 == 0 else t_len
    nb = t_len // qb
    k_chunk = jnp.arange(l_keys) // CHUNK
    group = H_C // H_KV

    def blocks(t):
        return t.reshape((bsz, nb, qb) + t.shape[2:]).swapaxes(0, 1)

    def one_block(args):
        qq, qqi, ww, qp = args
        q_chunk = qp // CHUNK
        adm = k_chunk[None, :] <= q_chunk[:, None]
        logits = jnp.einsum('bqjd,bsd->bqjs', qqi, ki).astype(jnp.float32)
        score = jnp.einsum('bqj,bqjs->bqs', ww.astype(jnp.float32), jax.nn.relu(logits))
        score = jnp.where(adm[None], score, -jnp.inf)
        _, idx = lax.top_k(score, n_sel)
        valid = k_chunk[idx] <= q_chunk[None, :, None]
        k_sel = jax.vmap(lambda kb_, ib_: kb_[ib_])(k, idx)
        v_sel = jax.vmap(lambda vb_, ib_: vb_[ib_])(v, idx)
        qg = qq.reshape(bsz, qb, H_KV, group, DH_C)
        s = jnp.einsum('bqhgd,bqshd->bqhgs', qg, k_sel).astype(jnp.float32) * (DH_C ** -0.5)
        s = jnp.where(valid[:, :, None, None, :], s, -jnp.inf)
        prob = jax.nn.softmax(s, axis=-1).astype(v.dtype)
        o = jnp.einsum('bqhgs,bqshd->bqhgd', prob, v_sel)
        return o.reshape(bsz, qb, H_C, DH_C)

    out = lax.map(one_block, (blocks(q), blocks(qi), blocks(wi), q_pos.reshape(nb, qb)))
    return out.swapaxes(0, 1).reshape(bsz, t_len, H_C, DH_C)


def hybrid_layer(x, p, conv_a_hist, delta_s, conv_b_hist, past_k, past_v, past_ki):
    bsz, t_len, _ = x.shape
    past_len = past_k.shape[1]
    pos = past_len + jnp.arange(t_len)
    h = rms_norm(x, p['norm_g'])
    proj = h @ p['w_in']
    (qkv_a, a_in, b_in, z_a, glu_b, z_b, q_c, k_c, v_c, qi_c, ki_c, wi_c, z_c, gate_in) = split_cols(proj, IN_SIZES)

    qkv_conv, conv_a_new = causal_dwconv(qkv_a, conv_a_hist, p['conv_a_w'])
    qkv_conv = jax.nn.silu(qkv_conv)
    q_a, k_a, v_a = jnp.split(qkv_conv, 3, axis=-1)
    q_a = l2_norm(q_a.reshape(bsz, t_len, H_A, DK_A)) * (DK_A ** -0.5)
    k_a = l2_norm(k_a.reshape(bsz, t_len, H_A, DK_A))
    v_a = v_a.reshape(bsz, t_len, H_A, DV_A)
    decay = -jnp.exp(p['a_log']) * jax.nn.softplus(a_in + p['dt_bias'])
    beta_a = jax.nn.sigmoid(b_in)
    o_a, delta_new = gated_delta_rule(q_a, k_a, v_a, decay, beta_a, delta_s)
    o_a = rms_norm(o_a, p['onorm_a_g']).reshape(bsz, t_len, D_A) * jax.nn.silu(z_a)
    y_a = o_a @ p['w_o_a']

    u_b = glu_b[..., :C_B] * jax.nn.sigmoid(glu_b[..., C_B:])
    u_b, conv_b_new = causal_dwconv(u_b, conv_b_hist, p['conv_b_w'])
    u_b = jax.nn.silu(layer_norm(u_b + p['conv_b_bias'], p['ln_b_g'], p['ln_b_b']))
    y_b = (u_b * jax.nn.silu(z_b)) @ p['w_pw2_b']

    q_c = partial_rope(q_c.reshape(bsz, t_len, H_C, DH_C), pos)
    k_c = partial_rope(k_c.reshape(bsz, t_len, H_KV, DH_C), pos)
    v_c = v_c.reshape(bsz, t_len, H_KV, DH_C)
    qi_c = partial_rope(qi_c.reshape(bsz, t_len, H_I, D_I), pos) * (D_I ** -0.5)
    ki_c = partial_rope(ki_c[:, :, None, :], pos)[:, :, 0, :]
    wi_c = wi_c * (H_I ** -0.5)
    k_all = jnp.concatenate([past_k.astype(k_c.dtype), k_c], axis=1)
    v_all = jnp.concatenate([past_v.astype(v_c.dtype), v_c], axis=1)
    ki_all = jnp.concatenate([past_ki.astype(ki_c.dtype), ki_c], axis=1)
    o_c = dsa_attention(q_c, qi_c, wi_c, k_all, v_all, ki_all, pos)
    y_c = (o_c.reshape(bsz, t_len, D_C) * jax.nn.silu(z_c)) @ p['w_o_c']

    m_a, m_b, m_c = jnp.split(jax.nn.sigmoid(gate_in), 3, axis=-1)
    out = (m_a * y_a + m_b * y_b + m_c * y_c) @ p['w_out']
    return x + out, (k_c, v_c, ki_c, conv_a_new, delta_new, conv_b_new)


def setup_inputs(seed: int = 0) -> dict:
    key = jax.random.key(seed)
    ks = jax.random.split(key, 24)
    f32 = jnp.float32
    nrm = jax.random.normal
    x_prompt = nrm(ks[0], (BATCH, SEQ, D_MODEL), f32)
    x_sample = nrm(ks[1], (DEC_BATCH, DEC_SEQ, D_MODEL), f32)
    cache_k = nrm(ks[2], (DEPTH, DEC_BATCH, PAST_LEN, H_KV, DH_C), f32)
    cache_v = nrm(ks[3], (DEPTH, DEC_BATCH, PAST_LEN, H_KV, DH_C), f32)
    cache_idx_k = nrm(ks[4], (DEPTH, DEC_BATCH, PAST_LEN, D_I), f32)
    state_conv_a = nrm(ks[5], (DEPTH, DEC_BATCH, SHORT_CONV - 1, 3 * D_A), f32)
    state_delta = 0.1 * nrm(ks[6], (DEPTH, DEC_BATCH, H_A, DK_A, DV_A), f32)
    state_conv_b = 0.5 * nrm(ks[7], (DEPTH, DEC_BATCH, CONV_B - 1, C_B), f32)
    norm_g = 1.0 + 0.02 * nrm(ks[8], (DEPTH, D_MODEL), f32)
    w_in = nrm(ks[9], (DEPTH, D_MODEL, N_IN), f32) * (D_MODEL ** -0.5)
    conv_a_w = nrm(ks[10], (DEPTH, SHORT_CONV, 3 * D_A), f32) * (SHORT_CONV ** -0.5)
    a_log = jnp.log(jax.random.uniform(ks[11], (DEPTH, H_A), f32, 1.0, 16.0))
    dt = jnp.exp(jax.random.uniform(ks[12], (DEPTH, H_A), f32, math.log(1e-3), math.log(1e-1)))
    dt_bias = dt + jnp.log(-jnp.expm1(-dt))
    onorm_a_g = 1.0 + 0.02 * nrm(ks[13], (DEPTH, DV_A), f32)
    w_o_a = nrm(ks[14], (DEPTH, D_A, D_MODEL), f32) * (D_A ** -0.5)
    conv_b_w = nrm(ks[15], (DEPTH, CONV_B, C_B), f32) * (CONV_B ** -0.5)
    conv_b_bias = 0.02 * nrm(ks[16], (DEPTH, C_B), f32)
    ln_b_g = 1.0 + 0.02 * nrm(ks[17], (DEPTH, C_B), f32)
    ln_b_b = 0.02 * nrm(ks[18], (DEPTH, C_B), f32)
    w_pw2_b = nrm(ks[19], (DEPTH, C_B, D_MODEL), f32) * (C_B ** -0.5)
    w_o_c = nrm(ks[20], (DEPTH, D_C, D_MODEL), f32) * (D_C ** -0.5)
    w_out = nrm(ks[21], (DEPTH, D_MODEL, D_MODEL), f32) * (D_MODEL ** -0.5)
    final_norm_g = 1.0 + 0.02 * nrm(ks[22], (D_MODEL,), f32)
    return {'x_prompt': x_prompt, 'x_sample': x_sample, 'cache_k': cache_k, 'cache_v': cache_v,
            'cache_idx_k': cache_idx_k, 'state_conv_a': state_conv_a, 'state_delta': state_delta,
            'state_conv_b': state_conv_b, 'norm_g': norm_g, 'w_in': w_in, 'conv_a_w': conv_a_w,
            'a_log': a_log, 'dt_bias': dt_bias, 'onorm_a_g': onorm_a_g, 'w_o_a': w_o_a,
            'conv_b_w': conv_b_w, 'conv_b_bias': conv_b_bias, 'ln_b_g': ln_b_g, 'ln_b_b': ln_b_b,
            'w_pw2_b': w_pw2_b, 'w_o_c': w_o_c, 'w_out': w_out, 'final_norm_g': final_norm_g}


def _stack(states, i):
    return jnp.stack([s[i] for s in states], axis=0)


def reference(x_prompt, x_sample, cache_k, cache_v, cache_idx_k, state_conv_a, state_delta, state_conv_b,
              norm_g, w_in, conv_a_w, a_log, dt_bias, onorm_a_g, w_o_a, conv_b_w, conv_b_bias,
              ln_b_g, ln_b_b, w_pw2_b, w_o_c, w_out, final_norm_g):
    dt = x_prompt.dtype
    bp = x_prompt.shape[0]
    zero_conv_a = jnp.zeros((bp, SHORT_CONV - 1, 3 * D_A), dt)
    zero_delta = jnp.zeros((bp, H_A, DK_A, DV_A), state_delta.dtype)
    zero_conv_b = jnp.zeros((bp, CONV_B - 1, C_B), dt)
    empty_kv = jnp.zeros((bp, 0, H_KV, DH_C), dt)
    empty_ki = jnp.zeros((bp, 0, D_I), dt)
    xp, xs = x_prompt, x_sample
    st_p, st_s = [], []
    for l in range(DEPTH):
        p = {'norm_g': norm_g[l], 'w_in': w_in[l], 'conv_a_w': conv_a_w[l], 'a_log': a_log[l],
             'dt_bias': dt_bias[l], 'onorm_a_g': onorm_a_g[l], 'w_o_a': w_o_a[l], 'conv_b_w': conv_b_w[l],
             'conv_b_bias': conv_b_bias[l], 'ln_b_g': ln_b_g[l], 'ln_b_b': ln_b_b[l], 'w_pw2_b': w_pw2_b[l],
             'w_o_c': w_o_c[l], 'w_out': w_out[l]}
        xp, sp = hybrid_layer(xp, p, zero_conv_a, zero_delta, zero_conv_b, empty_kv, empty_kv, empty_ki)
        xs, ss = hybrid_layer(xs, p, state_conv_a[l], state_delta[l], state_conv_b[l],
                              cache_k[l], cache_v[l], cache_idx_k[l])
        st_p.append(sp)
        st_s.append(ss)
    y_prompt = rms_norm(xp, final_norm_g)
    y_sample = rms_norm(xs, final_norm_g)
    new_k_prompt = _stack(st_p, 0)
    new_v_prompt = _stack(st_p, 1)
    new_idx_k_prompt = _stack(st_p, 2)
    conv_a_prompt = _stack(st_p, 3)
    delta_prompt = _stack(st_p, 4)
    conv_b_prompt = _stack(st_p, 5)
    new_k_sample = _stack(st_s, 0)
    new_v_sample = _stack(st_s, 1)
    new_idx_k_sample = _stack(st_s, 2)
    conv_a_sample = _stack(st_s, 3)
    delta_sample = _stack(st_s, 4)
    conv_b_sample = _stack(st_s, 5)
    return (y_prompt, y_sample, new_k_prompt, new_v_prompt, new_idx_k_prompt, conv_a_prompt, delta_prompt,
            conv_b_prompt, new_k_sample, new_v_sample, new_idx_k_sample, conv_a_sample, delta_sample,
            conv_b_sample)
```

```python
import contextlib
import numpy as np
import concourse.bass as bass
import concourse.mybir as mybir
from concourse.bass_utils import run_bass_kernel_spmd

F32 = mybir.dt.float32
BF16 = mybir.dt.bfloat16
AF = mybir.ActivationFunctionType
ALU = mybir.AluOpType
AX = mybir.AxisListType

D = 1024
NIN = 13400
EPS = 1e-6
O_QKV, O_A, O_B, O_ZA, O_GLU, O_ZB, O_QC, O_KC, O_VC, O_QI, O_KI, O_WI, O_ZC, O_GATE = (
    0, 3072, 3080, 3088, 4112, 6160, 7184, 8208, 8464, 8720, 9232, 9296, 9304, 10328)
NEG = -1.0e30


class Cfg:
    def __init__(self, depth=4, tp=4096, past=1024, ts=32, nss=2, sb=512, rounds=16):
        self.depth, self.tp, self.past, self.ts, self.nss, self.sb, self.rounds = depth, tp, past, ts, nss, sb, rounds
        self.lks = past + ts
        self.nsel_p = min(256, tp // 4)
        self.nsel_s = min(256, self.lks // 4)
        self.stages = 99


class Res:
    __slots__ = ("w", "rs", "excl")

    def __init__(self, excl=False, rs=None):
        self.w = {}
        self.rs = dict(rs) if rs else {}
        self.excl = excl


class TV:
    __slots__ = ("res", "ap")

    def __init__(self, res, ap):
        self.res = res
        self.ap = ap

    def __getitem__(self, idx):
        return TV(self.res, self.ap[idx])

    def v(self, fn):
        return TV(self.res, fn(self.ap))


class Eng:
    def __init__(self, h, sem):
        self.h = h
        self.sem = sem
        self.cnt = 0
        self.known = {}


class DQ:
    def __init__(self, sems):
        self.sems = sems
        self.i = 0
        self.vals = {s: 0 for s in sems}


def _ap(x):
    return x.ap if isinstance(x, TV) else x


class KB:
    def __init__(self, nc, es, cfg):
        self.nc, self.es, self.c = nc, es, cfg
        self.E = {}
        for name, h in (("pe", nc.tensor), ("act", nc.scalar), ("dve", nc.vector), ("pool", nc.gpsimd), ("sp", nc.sync)):
            self.E[name] = Eng(h, es.enter_context(nc.semaphore("s_" + name)))
        self.dq = {}
        for name in ("sp", "pool", "act"):
            self.dq[name] = DQ([es.enter_context(nc.semaphore("d_%s%d" % (name, i))) for i in range(8)])
        self.n_ins = 0

    def sb(self, name, shape, dt=F32, es=None):
        self.uid = getattr(self, "uid", 0) + 1
        name = "%s_%d" % (name, self.uid)
        t = (es or self.es).enter_context(self.nc.sbuf_tensor(name, list(shape), dt))
        return TV(Res(rs=getattr(self, "join", None)), t[:])

    def ps(self, name, shape, dt=F32):
        t = self.es.enter_context(self.nc.psum_tensor(name, list(shape), dt))
        return TV(Res(excl=True), t[:])

    def dram(self, name, shape, kind, dt=F32):
        return TV(Res(), self.nc.dram_tensor(name, list(shape), dt, kind=kind).ap())

    def _emit(self, en, fn, reads, writes, dma=False, par=False):
        E = self.E[en]
        need = {}

        def add1(s, v):
            if en == "pe" and s is E.sem:
                return
            if need.get(s, 0) < v:
                need[s] = v

        def add(evs):
            for s, v in evs.items():
                add1(s, v)

        xr = [r for r in reads if r.res.excl]
        if xr:
            reads = [r for r in reads if not r.res.excl]
            writes = list(writes) + xr
        for r in reads:
            add(r.res.w)
        for w in writes:
            if not par:
                add(w.res.w)
            add(w.res.rs)
        for s, v in need.items():
            if E.known.get(s, 0) < v:
                E.h.wait_ge(s, v)
                E.known[s] = v
        ins = fn(E.h)
        self.n_ins += 1
        if dma:
            q = self.dq[en]
            sem = q.sems[q.i % len(q.sems)]
            q.i += 1
            q.vals[sem] += 16
            ev = (sem, q.vals[sem])
            ins.then_inc(sem, 16)
        else:
            E.cnt += 1
            ev = (E.sem, E.cnt)
            ins.then_inc(E.sem, 1)
        for r in reads:
            rs = r.res.rs
            if rs.get(ev[0], 0) < ev[1]:
                rs[ev[0]] = ev[1]
        for w in writes:
            if par:
                if w.res.w.get(ev[0], 0) < ev[1]:
                    w.res.w[ev[0]] = ev[1]
            else:
                w.res.w = {ev[0]: ev[1]}
                w.res.rs = {}

    def soft(self):
        j = {}
        for E in self.E.values():
            if E.cnt:
                j[E.sem] = E.cnt
        for q in self.dq.values():
            for s, v in q.vals.items():
                if v:
                    j[s] = v
        self.join = j

    def barrier(self):
        evs = [(E.sem, E.cnt) for E in self.E.values() if E.cnt]
        for q in self.dq.values():
            evs += [(s, v) for s, v in q.vals.items() if v]
        for E in self.E.values():
            for s, v in evs:
                if s is E.sem:
                    continue
                if E.known.get(s, 0) < v:
                    E.h.wait_ge(s, v)
                    E.known[s] = v

    def finish(self):
        E = self.E["sp"]
        for q in self.dq.values():
            for s, v in q.vals.items():
                if v and E.known.get(s, 0) < v:
                    E.h.wait_ge(s, v)
        for name, X in self.E.items():
            if name != "sp" and X.cnt:
                E.h.wait_ge(X.sem, X.cnt)

    def mm(self, out, lhsT, rhs, start=True, stop=True, **kw):
        self._emit("pe", lambda e: e.matmul(out.ap, lhsT=lhsT.ap, rhs=rhs.ap, start=start, stop=stop, **kw),
                   [lhsT, rhs], [out])

    def tr(self, out, in_, ident):
        self._emit("pe", lambda e: e.transpose(out.ap, in_.ap, ident.ap), [in_, ident], [out])

    def act(self, out, in_, func, bias=None, scale=None, accum=None):
        reads = [in_] + [x for x in (bias, scale) if isinstance(x, TV)]
        writes = [out] + ([accum] if accum is not None else [])
        kw = {}
        if bias is not None:
            kw["bias"] = _ap(bias)
        if scale is not None:
            kw["scale"] = _ap(scale)
        if accum is not None:
            kw["accum_out"] = accum.ap
        self._emit("act", lambda e: e.activation(out.ap, in_.ap, func, **kw), reads, writes)

    def ts(self, out, in0, s1, s2=None, op0=ALU.mult, op1=None, accum=None, eng="dve"):
        reads = [in0] + [x for x in (s1, s2) if isinstance(x, TV)]
        writes = [out] + ([accum] if accum is not None else [])
        kw = {}
        if op1 is not None:
            kw["op1"] = op1
        if accum is not None:
            kw["accum_out"] = accum.ap
        self._emit(eng, lambda e: e.tensor_scalar(out.ap, in0.ap, _ap(s1), _ap(s2), op0, **kw), reads, writes)

    def tt(self, out, in0, in1, op, eng="dve"):
        self._emit(eng, lambda e: e.tensor_tensor(out.ap, in0.ap, in1.ap, op), [in0, in1], [out])

    def stt(self, out, in0, scalar, in1, op0, op1):
        reads = [in0, in1] + ([scalar] if isinstance(scalar, TV) else [])
        self._emit("dve", lambda e: e.scalar_tensor_tensor(out.ap, in0.ap, _ap(scalar), in1.ap, op0, op1), reads, [out])

    def cp(self, out, in_, eng="dve"):
        if eng == "act":
            self._emit("act", lambda e: e.copy(out.ap, in_.ap), [in_], [out])
        else:
            self._emit(eng, lambda e: e.tensor_copy(out.ap, in_.ap), [in_], [out])

    def memset(self, out, val, eng="dve"):
        self._emit(eng, lambda e: e.memset(out.ap, val), [], [out])

    def recip(self, out, in_):
        self._emit("dve", lambda e: e.reciprocal(out.ap, in_.ap), [in_], [out])

    def reduce(self, out, in_, op, absval=False):
        self._emit("dve", lambda e: e.tensor_reduce(out.ap, in_.ap, AX.X, op, apply_absolute_value=absval), [in_], [out])

    def dma(self, out, in_, q="pool", par=False):
        self._emit(q, lambda e: e.dma_start(out=out.ap, in_=in_.ap), [in_], [out], dma=True, par=par)


def chunk_consts(tt, c):
    i = np.arange(tt)
    same = (i[:, None] // c) == (i[None, :] // c)
    out = np.zeros((128, 5, 128), np.float32)
    out[:tt, 0, :tt] = ((i[:, None] > i[None, :]) & same)
    out[:tt, 1, :tt] = ((i[:, None] <= i[None, :]) & same)
    out[:tt, 2, :tt] = (i[:, None] == (i[None, :] // c) * c + c - 1)
    nch = tt // c
    for k in range(nch):
        out[:tt, 3, k] = (i // c == k)
        out[:tt, 3, 8 + k] = (i == k * c + c - 1)
    return out


def rope_tables(pos):
    pos = np.asarray(pos, np.float64)
    t = len(pos)
    inv128 = 500000.0 ** (-np.arange(16) * 2.0 / 32)
    inv64 = 500000.0 ** (-np.arange(8) * 2.0 / 16)
    a128 = pos[:, None] * inv128[None, :]
    a64 = pos[:, None] * inv64[None, :]
    c128 = np.ones((128, t)); s128 = np.zeros((128, t))
    c128[0:16] = np.cos(a128).T; c128[16:32] = np.cos(a128).T
    s128[0:16] = -np.sin(a128).T; s128[16:32] = np.sin(a128).T
    c128 *= 128 ** -0.5; s128 *= 128 ** -0.5
    c64 = np.ones((128, t)); s64 = np.zeros((128, t))
    for b in (0, 64):
        c64[b:b + 8] = np.cos(a64).T; c64[b + 8:b + 16] = np.cos(a64).T
        s64[b:b + 8] = -np.sin(a64).T; s64[b + 8:b + 16] = np.sin(a64).T
    c64 *= 64 ** -0.5; s64 *= 64 ** -0.5
    fm = np.stack([c128, s128, c64, s64], 1).astype(np.float32)
    tk = np.zeros((t, 2, 2, 16)); tk[:, 0] = np.cos(a128)[:, None, :]; tk[:, 1] = np.sin(a128)[:, None, :]
    tki = np.zeros((t, 2, 8)); tki[:, 0] = np.cos(a64); tki[:, 1] = np.sin(a64)
    tm = np.concatenate([tk.reshape(t, 64), tki.reshape(t, 16)], 1).astype(np.float32)
    return fm, tm


def misc_consts():
    m = np.zeros((128, 8, 128), np.float32)
    m[:, 0] = np.eye(128)
    m[:, 1] = 1.0
    for mm_ in range(16):
        m[mm_ + 16, 2, mm_] = 1.0
        m[mm_, 2, mm_ + 16] = 1.0
    for b in (0, 64):
        for mm_ in range(8):
            m[b + mm_ + 8, 3, b + mm_] = 1.0
            m[b + mm_, 3, b + mm_ + 8] = 1.0
    r = np.arange(128)
    adm = (r[None, :] < 64) | (r[:, None] >= 64)
    m[:, 4] = adm
    m[:, 5] = np.where(adm, 0.0, NEG)
    return m


def build(cfg, schedule=None):
    c = cfg
    rec = []
    nc = bass.Bass("TRN2", target_bir_lowering=False)
    es = contextlib.ExitStack()
    with es:
        K = KB(nc, es, c)
        L, TP, NSS, TS, PAST = c.depth, c.tp, c.nss, c.ts, c.past
        TSS = NSS * TS
        NKB_P = TP // 128
        di = lambda n, s: K.dram(n, s, "ExternalInput")
        do = lambda n, s: K.dram(n, s, "ExternalOutput")
        xp = di("xp", [TP, D]); xs = di("xs", [TSS, D])
        ck = di("ck", [L, NSS, PAST, 256]); cv = di("cv", [L, NSS, PAST, 256]); cki = di("cki", [L, NSS, PAST, 64])
        sca = di("sca", [L, NSS, 128, 24, 3]); sdl = di("sdl", [L, NSS, 8, 128, 128]); scb = di("scb", [L, NSS, 128, 8, 30])
        w_in = di("w_in", [L, D, NIN])
        w_oa = di("w_oa", [L, D, D]); w_pb = di("w_pb", [L, D, D]); w_oc = di("w_oc", [L, D, D]); w_out = di("w_out", [L, D, D])
        normg = di("normg", [L, 128, D]); fng = di("fng", [128, D])
        convaw = di("convaw", [L, 128, 24, 4]); hp8 = di("hp8", [L, 128, 16]); onormg = di("onormg", [L, 128, 1])
        convbw = di("convbw", [L, 128, 8, 31]); bvec = di("bvec", [L, 128, 3, 8])
        cmisc = di("cmisc", [128, 8, 128]); cchp = di("cchp", [128, 5, 128]); cchs = di("cchs", [128, 5, 128])
        rfm_p = di("rfm_p", [128, 4, TP]); rtm_p = di("rtm_p", [TP, 80])
        rfm_s = di("rfm_s", [128, 4, TSS]); rtm_s = di("rtm_s", [TSS, 80])
        yp = do("yp", [TP, D]); ys = do("ys", [TSS, D])
        nkp = do("nkp", [L, TP, 256]); nvp = do("nvp", [L, TP, 256]); nkip = do("nkip", [L, TP, 64])
        cap = do("cap", [L, 128, 24, 3]); dlp = do("dlp", [L, 8, 128, 128]); cbp = do("cbp", [L, 128, 8, 30])
        nks = do("nks", [L, TSS, 256]); nvs = do("nvs", [L, TSS, 256]); nkis = do("nkis", [L, TSS, 64])
        cas = do("cas", [L, NSS, 128, 24, 3]); dls = do("dls", [L, NSS, 8, 128, 128]); cbs = do("cbs", [L, NSS, 128, 8, 30])
        xscp = [K.dram("xscp%d" % i, [TP, D], "Internal") for i in range(2)]
        xscs = [K.dram("xscs%d" % i, [TSS, D], "Internal") for i in range(2)]

        SB = c.sb
        cm = K.sb("cm", [128, 8, 128]); K.dma(cm, cmisc)
        cmb = K.sb("cmb", [128, 4, 128], BF16); K.dma(cmb, cmisc[:, 0:4, :], q="pool")
        identF, onesF = cm[:, 0, :], cm[:, 1, :]
        identB, onesB, p128T, p64T = cmb[:, 0, :], cmb[:, 1, :], cmb[:, 2, :], cmb[:, 3, :]
        adm01, admneg = cm[:, 4, :], cm[:, 5, :]
        cch = {"p": K.sb("cchp_s", [128, 5, 128]), "s": K.sb("cchs_s", [128, 5, 128])}
        K.dma(cch["p"], cchp); K.dma(cch["s"], cchs)
        hT = K.sb("hT", [128, 8, SB], BF16)
        NW = 4
        DEPTH = 2
        wb = [K.sb("wb%d" % i, [128, 8, 512], BF16) for i in range(NW)]
        wbf = {"in": [K.dram("wbf_in%d" % l_, [D, NIN], "Internal", BF16) for l_ in range(L)]}
        for nm in ("oa", "pb", "oc", "out"):
            wbf[nm] = [K.dram("wbf_%s%d" % (nm, l_), [D, D], "Internal", BF16) for l_ in range(L)]
        wf32 = {"in": w_in, "oa": w_oa, "pb": w_pb, "oc": w_oc, "out": w_out}
        dgd = [K.dram("dgd%d" % l_, [8, 128, 3968], "Internal", BF16) for l_ in range(L)]

        def convert_layer(l_):
            for nm, ncols in (("in", NIN), ("oa", D), ("pb", D), ("oc", D), ("out", D)):
                cstep = 3350 if nm == "in" else 1024
                for r0 in range(0, D, 128):
                    for c0 in range(0, ncols, cstep):
                        K.dma(wbf[nm][l_][r0:r0 + 128, c0:c0 + cstep], wf32[nm][l_][r0:r0 + 128, c0:c0 + cstep], q="pool", par=True)
        ogT = K.sb("ogT", [128, 8, SB], BF16)
        LKS = c.lks
        KTW = max(TP, NSS * LKS)
        kT = K.sb("kT", [128, 2, KTW], BF16)
        NVB = max(NKB_P, NSS * 9)
        vaug = K.sb("vaug", [128, NVB, 2, 130], BF16)
        kiT2 = K.sb("kiT2", [128, KTW], BF16)
        Sst = {"p": [K.sb("S_p", [128, 8, 128])], "s": [K.sb("S_s%d" % i, [128, 8, 128]) for i in range(NSS)]}
        S16 = {"p": [K.sb("S16_p", [128, 8, 128], BF16)], "s": [K.sb("S16_s%d" % i, [128, 8, 128], BF16) for i in range(NSS)]}
        hista = {"p": K.sb("hista_p", [128, 24, 1, 3]), "s": K.sb("hista_s", [128, 24, NSS, 3])}
        histb = {"p": K.sb("histb_p", [128, 8, 1, 30]), "s": K.sb("histb_s", [128, 8, NSS, 30])}
        gB = K.sb("gB", [128, D]); caw = K.sb("caw", [128, 24, 4]); hp = K.sb("hp", [128, 16]); ea = K.sb("ea", [128, 8])
        ong = K.sb("ong", [128, 1]); cbw = K.sb("cbw", [128, 8, 31]); bv = K.sb("bv", [128, 3, 8])
        rtm = K.sb("rtm", [128, 4, 80])
        wtok = K.sb("wtok", [128, 4, 8])
        GT = K.sb("gat", [128, 4, 12, 8])
        geB = K.sb("geB", [128, 4, 2, 8])
        sm = [K.sb("sm%d" % i, [128, 8]) for i in range(8)]
        col = [K.sb("col%d" % i, [128, 1]) for i in range(12)]
        PB = [K.ps("pb%d" % i, [128, 512]) for i in range(7)]
        TRB = K.ps("trb", [128, 1024], BF16)
        pbi = [0]

        def nextpb(lo=0, hi=7):
            pbi[0] = (pbi[0] + 1) % (hi - lo)
            return PB[lo + pbi[0]]

        K.memset(vaug[:, :, :, 128:130], 1.0)
        wbi = [0]

        issued = [0]

        def issue_w(k, pieces):
            dst = wb[k % NW]
            off = 0
            for (nm, l_, c0, n) in pieces:
                if nm == "dg":
                    K.dma(dst.v(lambda a: a.rearrange("p k n -> p (k n)"))[:, 0:3968], dgd[l_][c0], q="sp")
                    continue
                K.dma(dst[:, :, off:off + n], wbf[nm][l_][:, c0:c0 + n].v(lambda a: a.rearrange("(k p) n -> p k n", p=128)), q="sp")
                off += n

        def load_w(pieces):
            k = len(rec)
            rec.append(list(pieces))
            if schedule is None:
                issue_w(k, pieces)
            else:
                assert schedule[k] == list(pieces), (k, schedule[k], pieces)
                while issued[0] < min(len(schedule), k + DEPTH + 1):
                    issue_w(issued[0], schedule[issued[0]])
                    issued[0] += 1
            return wb[k % NW]

        def proj_fm(ps, w, c0, ncol, T, rhs=None):
            src = hT if rhs is None else rhs
            for k in range(8):
                K.mm(ps[0:ncol, 0:T], w[:, k, c0:c0 + ncol], src[:, k, 0:T], start=(k == 0), stop=(k == 7))

        def run_sb(l, kind, sbi):
            if kind == "p":
                nseq, tl, C, T, TT, tok0 = 1, SB, 64, SB, 128, sbi * SB
                xin = xp if l == 0 else xscp[(l - 1) % 2]
                xout = xscp[l % 2]; yout = yp
                nko, nvo, nkio = nkp, nvp, nkip
                rfd, rtd = rfm_p, rtm_p
                nsel = c.nsel_p
            else:
                nseq, tl, C, T, TT, tok0 = NSS, TS, TS, TSS, TS, 0
                xin = xs if l == 0 else xscs[(l - 1) % 2]
                xout = xscs[l % 2]; yout = ys
                nko, nvo, nkio = nks, nvs, nkis
                rfd, rtd = rfm_s, rtm_s
                nsel = c.nsel_s
            NT = T // TT
            nch = TT // C
            cc = cch[kind]
            Mstrict, Mupper, SelEnd = cc[:, 0, :], cc[:, 1, :], cc[:, 2, :]
            cmk = cc[:, 3, :]
            last = (l == L - 1)
            wl = w_in[l]

            def wcols(c0, n):
                return ("in", l, c0, n)

            for i in range(NT):
                K.dma(rtm[0:TT, i, :], rtd[tok0 + i * TT: tok0 + (i + 1) * TT, :])

            def lock(gens):
                gens = list(gens)
                while gens:
                    for g_ in list(gens):
                        try:
                            next(g_)
                        except StopIteration:
                            gens.remove(g_)

            with contextlib.ExitStack() as es0:
                xts = [K.sb("xt", [128, D], F32, es0) for _ in range(NT)]
                junks = [K.sb("junk0", [128, D], BF16, es0) for _ in range(2)]
                xns = [K.sb("xn", [128, D], BF16, es0) for _ in range(NT)]
                c3 = K.sb("c3", [128, NT, 3], F32, es0)

                def p0_gen(i):
                    xt, junk, xn = xts[i], junks[i % 2], xns[i]
                    ss, sd, rstd = c3[:, i, 0:1], c3[:, i, 1:2], c3[:, i, 2:3]
                    K.dma(xt[0:TT], xin[tok0 + i * TT: tok0 + (i + 1) * TT, :])
                    yield
                    K.act(junk[0:TT], xt[0:TT], AF.Square, accum=ss[0:TT])
                    yield
                    K.act(sd[0:TT], ss[0:TT], AF.Sqrt, scale=1.0 / D, bias=EPS)
                    yield
                    K.recip(rstd[0:TT], sd[0:TT])
                    yield
                    K.stt(xn[0:TT], xt[0:TT], rstd[0:TT], gB[0:TT], ALU.mult, ALU.mult)
                    yield
                    for f in range(8):
                        K.tr(TRB[:, f * 128: f * 128 + TT], xn[0:TT, f * 128:(f + 1) * 128], identB[0:TT, 0:TT])
                    K.cp(hT[:, :, i * TT:(i + 1) * TT],
                         TRB.v(lambda a: a.rearrange("p (f t) -> p f t", f=8))[:, :, 0:TT], eng=("act" if i % 2 else "dve"))
                    yield

                lock([p0_gen(i) for i in range(NT)])
                K.soft()
            if c.stages < 1:
                return
            with contextlib.ExitStack() as es1:
                wkv = load_w([wcols(O_KC, 512)])
                wsm = load_w([wcols(O_A, 16), wcols(O_KI, 72)])
                mk1 = lambda n, shp, dt=F32: K.sb(n, shp, dt, es1)

                def kv_gen(i):
                    kf = mk1("kf", [128, 256]); vf = mk1("vf", [128, 256]); kif = mk1("kif", [128, 64])
                    kb16 = mk1("kb16", [128, 256], BF16); ki2b = mk1("ki2b", [128, 128], BF16)
                    t1 = mk1("t1", [128, 2, 16]); t2 = mk1("t2", [128, 2, 16]); t3 = mk1("t3", [128, 2, 16]); t4 = mk1("t4", [128, 2, 16])
                    smi = [mk1("smi", [128, 8]) for _ in range(2)]
                    pkv = PB[i]
                    bB = PB[4 + i % 3]
                    o_ = (i // 3) * 128
                    psm = bB[:, o_:o_ + 88]
                    pg = bB[:, o_ + 88:o_ + 104]
                    pge = bB[:, o_ + 104:o_ + 120]
                    tsl = slice(i * TT, (i + 1) * TT)
                    for k in range(8):
                        K.mm(pkv[0:TT, 0:512], hT[:, k, tsl], wkv[:, k, :], start=(k == 0), stop=(k == 7))
                    for k in range(8):
                        K.mm(psm[0:TT, 0:88], hT[:, k, tsl], wsm[:, k, 0:88], start=(k == 0), stop=(k == 7))
                    yield
                    k3 = pkv[0:TT, 0:256].v(lambda a: a.rearrange("p (h d) -> p h d", h=2))
                    kf3 = kf[0:TT].v(lambda a: a.rearrange("p (h d) -> p h d", h=2))
                    tb = rtm[0:TT, i, 0:64].v(lambda a: a.rearrange("p (a h d) -> p a h d", a=2, h=2))
                    cs, sn = tb[:, 0], tb[:, 1]
                    x1, x2 = k3[:, :, 0:16], k3[:, :, 16:32]
                    K.cp(kf[0:TT], pkv[0:TT, 0:256], eng="act")
                    K.tt(t1[0:TT], x1, cs, ALU.mult); K.tt(t2[0:TT], x2, sn, ALU.mult)
                    K.tt(t3[0:TT], x2, cs, ALU.mult); K.tt(t4[0:TT], x1, sn, ALU.mult)
                    K.cp(vf[0:TT], pkv[0:TT, 256:512], eng="act")
                    g_ = lambda j: GT[0:TT, i, j, :]
                    xa, ax, e1, l1, spl, gg, beta, Gs, nG, eG, kds, eGe = [g_(j) for j in range(12)]
                    K.tt(xa, psm[0:TT, 0:8], hp[0:TT, 8:16], ALU.add)
                    K.act(beta, psm[0:TT, 8:16], AF.Sigmoid)
                    K.act(wtok[0:TT, i, :], psm[0:TT, 80:88], AF.Copy, scale=float(8 ** -0.5))
                    yield
                    K.tt(kf3[:, :, 0:16], t1[0:TT], t2[0:TT], ALU.subtract)
                    K.tt(kf3[:, :, 16:32], t3[0:TT], t4[0:TT], ALU.add)
                    K.dma(nvo[l, tok0 + i * TT: tok0 + (i + 1) * TT, :], vf[0:TT])
                    K.act(ax, xa, AF.Abs)
                    tbi = rtm[0:TT, i, 64:80].v(lambda a: a.rearrange("p (a d) -> p a d", a=2))
                    csi, sni = tbi[:, 0], tbi[:, 1]
                    kip = psm[0:TT, 16:80]
                    K.cp(kif[0:TT], kip, eng="act")
                    yield
                    K.dma(nko[l, tok0 + i * TT: tok0 + (i + 1) * TT, :], kf[0:TT])
                    K.cp(kb16[0:TT], kf[0:TT], eng="pool")
                    u1, u2, u3, u4 = t1[0:TT, 0, 0:8], t2[0:TT, 0, 0:8], t3[0:TT, 0, 0:8], t4[0:TT, 0, 0:8]
                    K.tt(u1, kip[:, 0:8], csi, ALU.mult); K.tt(u2, kip[:, 8:16], sni, ALU.mult)
                    K.tt(u3, kip[:, 8:16], csi, ALU.mult); K.tt(u4, kip[:, 0:8], sni, ALU.mult)
                    K.act(e1, ax, AF.Exp, scale=-1.0)
                    yield
                    K.tt(kif[0:TT, 0:8], u1, u2, ALU.subtract)
                    K.tt(kif[0:TT, 8:16], u3, u4, ALU.add)
                    K.act(l1, e1, AF.Ln, bias=1.0)
                    if kind == "p":
                        kcol = tok0 + i * TT
                        vblk = (tok0 + i * TT) // 128
                    else:
                        kcol = i * LKS + PAST
                        vblk = i * 9 + 8
                    K.cp(vaug[0:TT, vblk, :, 0:128], vf[0:TT].v(lambda a: a.rearrange("p (h d) -> p h d", h=2)), eng="pool")
                    yield
                    for h in range(2):
                        K.tr(TRB[:, h * 128: h * 128 + TT], kb16[0:TT, h * 128:(h + 1) * 128], identB[0:TT, 0:TT])
                    K.cp(kT[:, :, kcol:kcol + TT], TRB[:, 0:256].v(lambda a: a.rearrange("p (h t) -> p h t", h=2))[:, :, 0:TT], eng="act")
                    K.dma(nkio[l, tok0 + i * TT: tok0 + (i + 1) * TT, :], kif[0:TT])
                    K.cp(ki2b[0:TT, 0:64], kif[0:TT], eng="pool"); K.cp(ki2b[0:TT, 64:128], kif[0:TT], eng="pool")
                    K.stt(spl, xa, 0.0, l1, ALU.max, ALU.add)
                    yield
                    K.stt(gg, spl, -1.0, ea[0:TT], ALU.mult, ALU.mult)
                    yield
                    K.tr(TRB[:, 256: 256 + TT], ki2b[0:TT, :], identB[0:TT, 0:TT])
                    K.cp(kiT2[:, kcol:kcol + TT], TRB[:, 256:256 + TT], eng="dve")
                    K.mm(pg[0:TT, 0:8], Mupper[0:TT, 0:TT], gg)
                    yield
                    K.cp(Gs, pg[0:TT, 0:8])
                    yield
                    K.mm(pg[0:TT, 8:16], SelEnd[0:TT, 0:TT], Gs)
                    K.act(nG, Gs, AF.Copy, scale=-1.0)
                    K.act(eG, Gs, AF.Exp)
                    yield
                    K.tt(kds, pg[0:TT, 8:16], Gs, ALU.subtract)
                    K.act(eGe, pg[0:TT, 8:16], AF.Exp)
                    K.tt(xa, beta, eG, ALU.mult)
                    K.act(ax, beta, AF.Copy, scale=-1.0)
                    yield
                    K.act(kds, kds, AF.Exp)
                    for ch in range(nch):
                        K.ts(smi[ch][0:TT], eGe, cmk[0:TT, 8 + ch: 9 + ch], None, op0=ALU.mult)
                    yield
                    for ch in range(nch):
                        K.mm(pge[:, ch * 8:(ch + 1) * 8], onesF[0:TT, :], smi[ch][0:TT])
                    for ch in range(nch):
                        K.ts(GT[0:TT, i, 2 + ch, :], eG, cmk[0:TT, ch:ch + 1], None, op0=ALU.mult)
                        K.ts(GT[0:TT, i, 4 + ch, :], kds, cmk[0:TT, ch:ch + 1], None, op0=ALU.mult)
                    yield
                    K.cp(geB[:, i, 0:nch, :], pge[:, 0:nch * 8].v(lambda a: a.rearrange("p (c h) -> p c h", c=nch)))
                    yield

                lock([kv_gen(i) for i in range(NT)])
                K.soft()
            if c.stages < 2:
                return
            RE = lambda pat, **kw: (lambda a: a.rearrange(pat, **kw))
            ubT = ocT = None
            mults = ALU.mult

            def lock(gens):
                gens = list(gens)
                while gens:
                    for g_ in list(gens):
                        try:
                            next(g_)
                        except StopIteration:
                            gens.remove(g_)

            def chain(*gs):
                for g_ in gs:
                    yield from g_

            def lockgen(gens):
                gens = list(gens)
                while gens:
                    for g_ in list(gens):
                        try:
                            next(g_)
                        except StopIteration:
                            gens.remove(g_)
                    yield

            def phase_A():
                with contextlib.ExitStack() as ea_:
                    mk = lambda n, shp, dt=F32: K.sb(n, shp, dt, ea_)
                    qkvb = [[mk("qkv%d_%d" % (j, par), [128, T]) for j in range(3)] for par in range(2)]
                    qk16 = [[mk("qk16%d_%d" % (j, par), [128, T], BF16) for j in range(2)] for par in range(2)]
                    zgb = [mk("zg%d" % par, [128, T], BF16) for par in range(2)]
                    xc = mk("xc", [128, nseq, 3 + tl]); acc = mk("acc", [128, nseq, tl]); sl = mk("sl", [128, T])
                    sq = mk("sq", [128, T], BF16); sdd = mk("sdd", [128, T])
                    junkA = mk("junkA", [128, 128], BF16)
                    pre_f32 = ("t0", "E", "Ls", "Eu")
                    pre_b16 = ("vb", "kbg", "LF", "Fb", "Ma", "MTa", "Mb", "MTb", "Xa", "Xb")
                    pre = []
                    for i_ in range(NT):
                        dct = dict((n, mk("%s_%d" % (n, i_), [128, 128])) for n in pre_f32)
                        dct.update((n, mk("%s_%d" % (n, i_), [128, 128], BF16)) for n in pre_b16)
                        pre.append(dct)
                    hand = [[dict(u=mk("u", [128, 128]), wT=mk("wT", [128, 128], BF16), kd2=mk("kd2", [128, 2, 128], BF16),
                                  qk2=mk("qk2", [128, 2, 128], BF16))
                             for i_ in range(NT)] for par in range(2)]
                    scl = [dict(vn0=mk("vn0", [128, 128], BF16), vn1=mk("vn1", [128, 128], BF16),
                                oacc=mk("oacc", [128, 128]), onb=mk("onb", [128, 128], BF16),
                                c0=mk("c0", [128, 1]), c1=mk("c1", [128, 1]), c2=mk("c2", [128, 1])) for par in range(2)]
                    nlev = {64: 5, 32: 4}[C]

                    def pre_gen(h, i):
                        S = pre[i]
                        H = hand[h % 2][i]
                        qkv = qkvb[h % 2]
                        tsl = slice(i * TT, (i + 1) * TT)
                        g = lambda j: GT[0:TT, i, j, h:h + 1]
                        bexpG, negb, beta, Gc, nGc = g(0), g(1), g(6), g(7), g(8)
                        kdsc = [g(4 + ch) for ch in range(nch)]
                        iF = identF[0:TT, 0:TT]
                        sq_ = lambda t: t[0:TT, 0:TT]
                        p1 = PB[i]
                        K.tr(p1[0:TT, 0:128], qkv[2][:, tsl], identF)
                        K.tr(p1[0:TT, 128:256], qkv[1][:, tsl], identF)
                        q16, k16 = qk16[h % 2]
                        K.mm(p1[0:TT, 256:256 + TT], k16[:, tsl], k16[:, tsl])
                        K.mm(p1[0:TT, 384:384 + TT], k16[:, tsl], q16[:, tsl])
                        K.ts(sq_(S["t0"]), iF, nGc, None, op0=mults, eng="pool")
                        yield
                        K.ts(S["vb"][0:TT], p1[0:TT, 0:128], beta, None, op0=mults)
                        K.ts(S["kbg"][0:TT], p1[0:TT, 128:256], bexpG, None, op0=mults)
                        for ch in range(nch):
                            K.act(H["kd2"][0:TT, ch, :], p1[0:TT, 128:256], AF.Copy, scale=kdsc[ch])
                        p2 = PB[i]
                        K.mm(p2[0:TT, 0:TT], onesF[0:TT, 0:TT], sq_(S["t0"]))
                        yield
                        K.act(sq_(S["t0"]), p2[0:TT, 0:TT], AF.Abs, bias=Gc)
                        yield
                        K.act(sq_(S["E"]), sq_(S["t0"]), AF.Exp, scale=-1.0)
                        yield
                        K.tt(sq_(S["Ls"]), sq_(S["E"]), Mstrict[0:TT, 0:TT], mults)
                        K.tt(sq_(S["Eu"]), sq_(S["E"]), Mupper[0:TT, 0:TT], mults, eng="pool")
                        yield
                        K.stt(sq_(S["E"]), p1[0:TT, 256:256 + TT], negb, sq_(S["Ls"]), mults, mults)
                        K.tt(sq_(S["t0"]), p1[0:TT, 384:384 + TT], sq_(S["Eu"]), mults)
                        yield
                        for ch in range(nch):
                            K.ts(H["qk2"][0:TT, ch, 0:TT], sq_(S["t0"]), cmk[0:TT, ch:ch + 1], None, op0=mults, eng="pool")
                        p3 = PB[i][:, 128:256]
                        K.tr(p3[0:TT, 0:TT], sq_(S["E"]), iF)
                        K.cp(sq_(S["Ma"]), sq_(S["E"]), eng="pool")
                        yield
                        K.cp(sq_(S["MTa"]), p3[0:TT, 0:TT], eng="act")
                        K.tt(sq_(S["Xa"]), p3[0:TT, 0:TT], iF, ALU.add)
                        yield
                        M, MT, Mn, MTn, X, Xn = S["Ma"], S["MTa"], S["Mb"], S["MTb"], S["Xa"], S["Xb"]
                        pk = PB[i]
                        iB = identB[0:TT, 0:TT]
                        for kl in range(1, nlev + 2):
                            do_sq = kl <= nlev
                            lastl = (kl == nlev)
                            if kl > 1:
                                K.mm(pk[0:TT, 256:256 + TT], iB, sq_(X), start=True, stop=False)
                                K.mm(pk[0:TT, 256:256 + TT], sq_(M), sq_(X), start=False, stop=True)
                            if do_sq and not (lastl and False):
                                K.mm(pk[0:TT, 0:TT], sq_(MT), sq_(M))
                                if not lastl:
                                    K.mm(pk[0:TT, 128:128 + TT], sq_(M), sq_(MT))
                            yield
                            if kl > 1:
                                K.cp(sq_(Xn), pk[0:TT, 256:256 + TT])
                                X, Xn = Xn, X
                            if do_sq:
                                K.cp(sq_(Mn), pk[0:TT, 0:TT], eng="act")
                                if not lastl:
                                    K.cp(sq_(MTn), pk[0:TT, 128:128 + TT], eng="dve")
                                M, Mn = Mn, M
                                MT, MTn = MTn, MT
                            yield
                        p4 = PB[i]
                        K.mm(p4[0:TT, 0:128], sq_(X), S["vb"][0:TT])
                        K.mm(p4[:, 128:128 + TT], S["kbg"][0:TT], sq_(X))
                        yield
                        K.cp(H["u"][0:TT], p4[0:TT, 0:128], eng="act")
                        K.cp(H["wT"][:, 0:TT], p4[:, 128:128 + TT], eng="act")
                        yield

                    def scan_gen(h):
                        qkv = qkvb[h % 2]
                        zg = zgb[h % 2]
                        for i in range(NT):
                            H = dict(hand[h % 2][i])
                            H.update(scl[i % 2])
                            tsl = slice(i * TT, (i + 1) * TT)
                            g = lambda j: GT[0:TT, i, j, h:h + 1]
                            egc = [g(2 + ch) for ch in range(nch)]
                            vns = [H["vn0"], H["vn1"]]
                            for ch in range(nch):
                                St = (Sst["p"][0] if kind == "p" else Sst["s"][i])[:, h, :]
                                S16t = (S16["p"][0] if kind == "p" else S16["s"][i])[:, h, :]
                                p5 = PB[4]
                                K.mm(p5[0:TT, 0:128], H["wT"][:, 0:TT], S16t)
                                K.mm(p5[0:TT, 128:256], qk16[h % 2][0][:, tsl], S16t)
                                yield
                                K.tt(vns[ch][0:TT], H["u"][0:TT], p5[0:TT, 0:128], ALU.subtract)
                                if ch == 0:
                                    K.ts(H["oacc"][0:TT], p5[0:TT, 128:256], egc[0], None, op0=mults)
                                else:
                                    K.stt(H["oacc"][0:TT], p5[0:TT, 128:256], egc[ch], H["oacc"][0:TT], mults, ALU.add)
                                yield
                                K.mm(p5[:, 256:384], H["kd2"][0:TT, ch, :], vns[ch][0:TT])
                                yield
                                K.stt(S16t, St, geB[:, i, ch, h:h + 1], p5[:, 256:384], mults, ALU.add)
                                K.stt(St, St, geB[:, i, ch, h:h + 1], p5[:, 256:384], mults, ALU.add)
                                yield
                            p6 = PB[4][:, 384:512]
                            for ch in range(nch):
                                K.mm(p6[0:TT, 0:128], H["qk2"][0:TT, ch, 0:TT], vns[ch][0:TT], start=(ch == 0), stop=(ch == nch - 1))
                            yield
                            K.tt(H["oacc"][0:TT], H["oacc"][0:TT], p6[0:TT, 0:128], ALU.add)
                            yield
                            ssA, sdA, rsA = H["c0"], H["c1"], H["c2"]
                            K.act(junkA[0:TT], H["oacc"][0:TT], AF.Square, accum=ssA[0:TT])
                            yield
                            K.act(sdA[0:TT], ssA[0:TT], AF.Ln, scale=1.0 / 128, bias=EPS)
                            yield
                            K.act(rsA[0:TT], sdA[0:TT], AF.Exp, scale=-0.5)
                            yield
                            K.ts(H["onb"][0:TT], H["oacc"][0:TT], rsA[0:TT], None, op0=mults)
                            yield
                            K.tr(TRB[:, 0:TT], H["onb"][0:TT, :], identB[0:TT, 0:TT])
                            yield
                            K.stt(ogT[:, h, tsl], TRB[:, 0:TT], ong[:, 0:1], zg[:, tsl], mults, mults)
                            yield

                    def proj_gen(h):
                        qkv = qkvb[h % 2]
                        zg = zgb[h % 2]
                        w = load_w([wcols(O_QKV + h * 128, 128), wcols(O_QKV + 1024 + h * 128, 128),
                                    wcols(O_QKV + 2048 + h * 128, 128), wcols(O_ZA + h * 128, 128)])
                        for j in range(3):
                            pj = PB[5]
                            proj_fm(pj, w, j * 128, 128, T)
                            ct = j * 8 + h
                            K.cp(xc[:, :, 0:3], hista[kind][:, ct, :, :], eng="pool")
                            yield
                            K.cp(xc[:, :, 3:3 + tl], pj[:, 0:T].v(RE("p (s t) -> p s t", s=nseq)), eng="act")
                            yield
                            K.cp(hista[kind][:, ct, :, :], xc[:, :, tl:tl + 3], eng="pool")
                            K.ts(acc, xc[:, :, 0:tl], caw[:, ct, 0:1], None, op0=mults)
                            yield
                            for k in range(1, 4):
                                K.stt(acc, xc[:, :, k:k + tl], caw[:, ct, k:k + 1], acc, mults, ALU.add)
                                yield
                            accf = acc.v(RE("p s t -> p (s t)"))
                            if j < 2:
                                K.act(sl, accf, AF.Silu)
                                yield
                                K.act(sq, sl, AF.Square)
                                yield
                                pss = PB[6]
                                K.mm(pss[:, 0:T], onesB, sq)
                                yield
                                K.act(sdd, pss[:, 0:T], AF.Ln, bias=EPS)
                                yield
                                K.act(sdd, sdd, AF.Exp, scale=-0.5)
                                yield
                                K.stt(qkv[j], sl, float(128 ** -0.5) if j == 0 else 1.0, sdd, mults, mults)
                                yield
                                K.cp(qk16[h % 2][j], qkv[j], eng="pool")
                                yield
                            else:
                                K.act(qkv[2], accf, AF.Silu)
                                yield
                        pj = PB[6]
                        proj_fm(pj, w, 384, 128, T)
                        yield
                        K.act(zg, pj[:, 0:T], AF.Silu)
                        yield

                    def head_pre(h):
                        return chain(proj_gen(h), lockgen([pre_gen(h, i) for i in range(NT)]))

                    lock([head_pre(0)])
                    for h in range(8):
                        gens = [scan_gen(h)]
                        if h < 7:
                            gens.append(head_pre(h + 1))
                        lock(gens)
                    K.soft()

            def phase_B():
                with contextlib.ExitStack() as eb_:
                    mk = lambda n, shp, dt=F32: K.sb(n, shp, dt, eb_)
                    cb = mk("cb", [128, 8, T]); ub = mk("ub", [128, nseq, 30 + tl]); sg = mk("sg", [128, T])
                    sqb = mk("sqb", [128, T]); ub16 = mk("ub16", [128, nseq, 30 + tl], BF16)

                    mean = mk("mean", [128, T]); rstdb = mk("rstdb", [128, T]); nmr = mk("nmr", [128, T])
                    tmp = mk("tmpb", [128, T]); tmp2 = mk("tmpb2", [128, T])
                    for ct in range(8):
                        w = load_w([wcols(O_GLU + ct * 128, 128), wcols(O_GLU + 1024 + ct * 128, 128), wcols(O_ZB + ct * 128, 128)])
                        pa = nextpb(0, 5); proj_fm(pa, w, 0, 128, T)
                        pb_ = nextpb(0, 5); proj_fm(pb_, w, 128, 128, T)
                        K.act(sg, pb_[:, 0:T], AF.Sigmoid)
                        K.cp(ub[:, :, 0:30], histb[kind][:, ct, :, :], eng="pool")
                        K.tt(ub[:, :, 30:30 + tl], pa[:, 0:T].v(RE("p (s t) -> p s t", s=nseq)), sg.v(RE("p (s t) -> p s t", s=nseq)), mults)
                        K.cp(histb[kind][:, ct, :, :], ub[:, :, tl:tl + 30], eng="pool")
                        K.cp(ub16, ub, eng="pool")
                        dgw = load_w([("dg", l, ct, 3968)])
                        dg = dgw.v(lambda a: a.rearrange("p k n -> p (k n)"))[:, 0:3968].v(lambda a: a.rearrange("p (k m) -> p k m", k=31))
                        pc = nextpb(0, 5)
                        pc3 = pc[:, 0:T].v(RE("p (s t) -> p s t", s=nseq))
                        for k in range(31):
                            K.mm(pc3, dg[:, k, :], ub16[:, :, k:k + tl], start=(k == 0), stop=(k == 30))
                        K.act(cb[:, ct, :], pc[:, 0:T], AF.Identity, bias=bv[:, 0, ct:ct + 1])
                        K.act(sqb, cb[:, ct, :], AF.Square)
                        K.mm(PB[5][:, 0:T], onesF, cb[:, ct, :], start=(ct == 0), stop=(ct == 7))
                        K.mm(PB[6][:, 0:T], onesF, sqb, start=(ct == 0), stop=(ct == 7))
                        pz = nextpb(0, 5); proj_fm(pz, w, 256, 128, T)
                        K.act(ubT[:, ct, 0:T], pz[:, 0:T], AF.Silu)
                    K.act(mean, PB[5][:, 0:T], AF.Copy, scale=1.0 / D)
                    K.tt(tmp, mean, mean, mults)
                    K.stt(tmp2, PB[6][:, 0:T], 1.0 / D, tmp, mults, ALU.subtract)
                    K.act(tmp2, tmp2, AF.Sqrt, bias=EPS)
                    K.recip(rstdb, tmp2)
                    K.stt(nmr, mean, -1.0, rstdb, mults, mults)
                    for ct in range(8):
                        K.tt(tmp, cb[:, ct, :], rstdb, mults)
                        K.tt(tmp, tmp, nmr, ALU.add)
                        K.ts(tmp, tmp, bv[:, 1, ct:ct + 1], bv[:, 2, ct:ct + 1], op0=mults, op1=ALU.add)
                        K.act(tmp2, tmp, AF.Silu)
                        K.tt(ubT[:, ct, 0:T], tmp2, ubT[:, ct, 0:T], mults)
                    K.soft()

            def phase_C():
                with contextlib.ExitStack() as ec_:
                    mk = lambda n, shp, dt=F32: K.sb(n, shp, dt, ec_)
                    qT = mk("qT", [128, 8, T], BF16); qiT = mk("qiT", [128, 4, T], BF16)
                    R = c.rounds
                    ec2 = contextlib.ExitStack()
                    mk2 = lambda n, shp, dt=F32: K.sb(n, shp, dt, ec2)
                    xb = mk2("xb", [128, T], BF16); r1 = mk2("r1", [128, T]); r2 = mk2("r2", [128, T])
                    rfm = mk2("rfm", [128, 4, T])
                    K.dma(rfm, rfd[:, :, tok0:tok0 + T])
                    if kind == "s":
                        kst = mk2("kst", [128, 8, 256], BF16); kist = mk2("kist", [128, 8, 128], BF16)
                        for s_ in range(nseq):
                            K.dma(kst, ck[l, s_].v(RE("(b p) f -> p b f", p=128)), q="pool")
                            for b in range(8):
                                K.dma(vaug[:, s_ * 9 + b, :, 0:128],
                                      cv[l, s_, b * 128:(b + 1) * 128, :].v(RE("p (h d) -> p h d", h=2)), q="pool")
                            K.dma(kist[:, :, 0:64], cki[l, s_].v(RE("(b p) f -> p b f", p=128)), q="pool")
                            K.dma(kist[:, :, 64:128], cki[l, s_].v(RE("(b p) f -> p b f", p=128)), q="pool")
                            for b in range(8):
                                for h in range(2):
                                    K.tr(TRB[:, h * 128:(h + 1) * 128], kst[:, b, h * 128:(h + 1) * 128], identB)
                                K.tr(TRB[:, 256:384], kist[:, b, :], identB)
                                c0 = s_ * LKS + b * 128
                                K.cp(kT[:, :, c0:c0 + 128], TRB[:, 0:256].v(RE("p (h t) -> p h t", h=2)), eng="act")
                                K.cp(kiT2[:, c0:c0 + 128], TRB[:, 256:384])
                    for h in range(8):
                        if h % 4 == 0:
                            w = load_w([wcols(O_QC + h * 128, 512)])
                        pq = nextpb(); proj_fm(pq, w, (h % 4) * 128, 128, T)
                        K.cp(xb, pq[:, 0:T], eng="act")
                        pr = nextpb(); K.mm(pr[:, 0:T], p128T, xb)
                        K.tt(r1, pq[:, 0:T], rfm[:, 0, 0:T], mults)
                        K.tt(r2, pr[:, 0:T], rfm[:, 1, 0:T], mults)
                        K.tt(qT[:, h, :], r1, r2, ALU.add)
                    w = load_w([wcols(O_QI, 512)])
                    for j in range(4):
                        pq = nextpb(); proj_fm(pq, w, j * 128, 128, T)
                        K.cp(xb, pq[:, 0:T], eng="act")
                        pr = nextpb(); K.mm(pr[:, 0:T], p64T, xb)
                        K.tt(r1, pq[:, 0:T], rfm[:, 2, 0:T], mults)
                        K.tt(r2, pr[:, 0:T], rfm[:, 3, 0:T], mults)
                        K.tt(qiT[:, j, :], r1, r2, ALU.add)
                    for h in range(8):
                        if h % 4 == 0:
                            w = load_w([wcols(O_ZC + h * 128, 512)])
                        pz = nextpb(); proj_fm(pz, w, (h % 4) * 128, 128, T)
                        K.act(ocT[:, h, 0:T], pz[:, 0:T], AF.Silu)
                    ec2.close()
                    K.soft()
                    NKMAX = max(TP if kind == "p" else 0, LKS + 96)
                    NKMAX = ((NKMAX + 127) // 128) * 128
                    sc = mk("sc", [128, NKMAX]); cmpb = mk("cmpb", [128, NKMAX], BF16)
                    maskT = mk("maskT", [128, NKMAX // 128, 128], BF16)
                    rl = [mk("rl%d" % i_, [128, 512]) for i_ in range(2)]
                    pt = [mk("pt%d" % i_, [128, 4, 128], BF16) for i_ in range(3)]
                    ob = [mk("ob%d" % i_, [128, 128], BF16) for i_ in range(2)]
                    dtab = mk("dtab", [128, R + 2]); ndt = mk("ndt", [128, R + 2])
                    PV = [PB[4], PB[5], PB[6]]
                    Bc, d0, mid, cnt, uu, thr, rden = col[6], col[7], col[8], col[9], col[10], col[11], col[0]
                    maskTs = [maskT, mk("maskT2", [128, NKMAX // 128, 128], BF16)]

                    def tile_info(i):
                        if kind == "p":
                            qt = tok0 // 128 + i
                            blocks = [(b * 128, 128, b) for b in range(qt + 1)]
                            return blocks, (qt + 1) * 128, True
                        blocks = [(i * LKS + b * 128, 128, i * 9 + b) for b in range(8)] + [(i * LKS + PAST, TS, i * 9 + 8)]
                        return blocks, LKS, False

                    def X_gen(i):
                        qn = TT
                        qsl = slice(i * TT, (i + 1) * TT)
                        blocks, N, masked = tile_info(i)
                        mT_ = maskTs[i % 2]
                        col0 = blocks[0][0]
                        for sb0 in range(0, N, 512):
                            n = min(512, N - sb0)
                            for j in range(8):
                                pl = nextpb(0, 2)
                                hr = slice((j % 2) * 64, (j % 2) * 64 + 64)
                                K.mm(pl[0:qn, 0:n], qiT[hr, j // 2, qsl], kiT2[hr, col0 + sb0: col0 + sb0 + n])
                                r = rl[j % 2]
                                if j % 2 == 0:
                                    K.act(r[0:qn, 0:n], pl[0:qn, 0:n], AF.Relu)
                                    if j == 0:
                                        K.ts(sc[0:qn, sb0:sb0 + n], r[0:qn, 0:n], wtok[0:qn, i, 0:1], None, op0=mults)
                                    else:
                                        K.stt(sc[0:qn, sb0:sb0 + n], r[0:qn, 0:n], wtok[0:qn, i, j:j + 1], sc[0:qn, sb0:sb0 + n], mults, ALU.add)
                                else:
                                    K.ts(r[0:qn, 0:n], pl[0:qn, 0:n], 0.0, wtok[0:qn, i, j:j + 1], op0=ALU.max, op1=mults)
                                    K.tt(sc[0:qn, sb0:sb0 + n], sc[0:qn, sb0:sb0 + n], r[0:qn, 0:n], ALU.add, eng="pool")
                                yield
                        lastc = slice(N - 128, N)
                        if masked:
                            K.tt(sc[0:qn, lastc], sc[0:qn, lastc], adm01[0:qn, :], mults)
                        K.reduce(Bc[0:qn], sc[0:qn, 0:N], ALU.max, absval=True)
                        yield
                        K.ts(d0[0:qn], Bc[0:qn], 1.0, None, op0=ALU.add)
                        if masked:
                            K.tt(sc[0:qn, lastc], sc[0:qn, lastc], admneg[0:qn, :], ALU.add)
                        yield
                        for r_ in range(R + 1):
                            K.ts(dtab[0:qn, r_:r_ + 1], d0[0:qn], float(2.0 ** -r_), None, op0=mults, eng="pool")
                        K.memset(mid[0:qn], 0.0)
                        K.ts(ndt[0:qn, 0:R + 1], dtab[0:qn, 0:R + 1], -1.0, None, op0=mults, eng="pool")
                        yield
                        thrc = float(2 * nsel - N)
                        for r_ in range(R):
                            K.act(cmpb[0:qn, 0:N], sc[0:qn, 0:N], AF.Sign, bias=mid[0:qn], accum=cnt[0:qn])
                            yield
                            K.act(uu[0:qn], cnt[0:qn], AF.Sign, bias=float(0.5 - thrc))
                            yield
                            K.act(mid[0:qn], uu[0:qn], AF.Identity, scale=ndt[0:qn, r_ + 1:r_ + 2], bias=mid[0:qn])
                            yield
                        K.tt(thr[0:qn], mid[0:qn], dtab[0:qn, R:R + 1], ALU.add)
                        yield
                        K.ts(cmpb[0:qn, 0:N], sc[0:qn, 0:N], thr[0:qn], 0.0, op0=ALU.add, op1=ALU.is_ge)
                        yield
                        nfull = sum(1 for b in blocks if b[1] == 128)
                        for g0 in range(0, nfull, 8):
                            m = min(8, nfull - g0)
                            for bb in range(m):
                                K.tr(TRB[:, bb * 128: bb * 128 + qn], cmpb[0:qn, (g0 + bb) * 128:(g0 + bb + 1) * 128], identB[0:qn, 0:qn])
                            K.cp(mT_[:, g0:g0 + m, 0:qn], TRB.v(RE("p (b t) -> p b t", b=8))[:, 0:m, 0:qn],
                                 eng=("act" if (g0 // 8) % 2 else "dve"))
                            yield
                        for bi, (kc0, kn, vblk) in enumerate(blocks):
                            if kn != 128:
                                K.tr(TRB[0:kn, 0:qn], cmpb[0:qn, bi * 128: bi * 128 + kn], identB[0:qn, 0:qn])
                                K.cp(mT_[0:kn, bi, 0:qn], TRB[0:kn, 0:qn])
                                yield

                    def Y_gen(i):
                        qn = TT
                        qsl = slice(i * TT, (i + 1) * TT)
                        blocks, N, masked = tile_info(i)
                        mT_ = maskTs[i % 2]
                        first = [True, True, True]
                        nblk = len(blocks)
                        steps = [(bi, g) for bi in range(nblk) for g in range(2)]

                        def stA(k):
                            bi, g = steps[k]
                            kc0, kn, vblk = blocks[bi]
                            st = nextpb(2, 4)
                            st3 = st[0:kn, 0:4 * qn].v(RE("p (h t) -> p h t", h=4))
                            K.mm(st3, kT[:, g, kc0:kc0 + kn], qT[:, 4 * g:4 * g + 4, qsl])
                            p = pt[k % 3]
                            K.act(p[0:kn, :, 0:qn], st3, AF.Exp)
                            K.tt(p[0:kn, :, 0:qn], p[0:kn, :, 0:qn],
                                 mT_[0:kn, bi, 0:qn].v(lambda a: a.unsqueeze(1).broadcast_to([kn, 4, qn])), mults,
                                 eng=("pool" if k % 2 else "dve"))

                        def stB(k):
                            bi, g = steps[k]
                            kc0, kn, vblk = blocks[bi]
                            p = pt[k % 3]
                            for hh in range(4):
                                h = 4 * g + hh
                                bk = h // 3
                                off = (h % 3) * 129
                                K.mm(PV[bk][0:qn, off:off + 129], p[0:kn, hh, 0:qn], vaug[0:kn, vblk, g, 0:129],
                                     start=first[bk], stop=(bi == nblk - 1), skip_group_check=True)
                                first[bk] = False

                        for k in range(len(steps)):
                            stA(k)
                            if k > 0:
                                stB(k - 1)
                            yield
                        stB(len(steps) - 1)
                        yield
                        for h in range(8):
                            bk = h // 3
                            off = (h % 3) * 129
                            o_ = ob[h % 2]
                            K.recip(rden[0:qn], PV[bk][0:qn, off + 128:off + 129])
                            K.ts(o_[0:qn], PV[bk][0:qn, off:off + 128], rden[0:qn], None, op0=mults)
                            K.tr(TRB[:, h * 128:h * 128 + qn], o_[0:qn, :], identB[0:qn, 0:qn])
                            K.tt(ocT[:, h, qsl], TRB[:, h * 128:h * 128 + qn], ocT[:, h, qsl], mults)
                            yield

                    lock([X_gen(0)])
                    for i in range(NT):
                        gens = [Y_gen(i)]
                        if i + 1 < NT:
                            gens.append(X_gen(i + 1))
                        lock(gens)
                    K.soft()

            def phase_M():
                with contextlib.ExitStack() as em_:
                    mk = lambda n, shp, dt=F32: K.sb(n, shp, dt, em_)
                    sga = mk("sga", [128, T]); macc = mk("macc", [128, T]); tmp = mk("tmpm", [128, T])
                    xt = mk("xtm", [128, D]); xo = mk("xo", [128, D]); junk = mk("junkm", [128, D], BF16)
                    mT = mk("mT", [128, 8, SB], BF16)
                    if last:
                        fngB = mk("fngB", [128, D])
                        K.dma(fngB, fng)
                    for n in range(8):
                        wy = load_w([("oa", l, n * 128, 128), ("pb", l, n * 128, 128), ("oc", l, n * 128, 128)])
                        wg = load_w([wcols(O_GATE + n * 128, 128), wcols(O_GATE + 1024 + n * 128, 128), wcols(O_GATE + 2048 + n * 128, 128)])
                        for bi, src in enumerate((ogT, ubT, ocT)):
                            py = nextpb(); proj_fm(py, wy, bi * 128, 128, T, rhs=src)
                            pg = nextpb(); proj_fm(pg, wg, bi * 128, 128, T)
                            K.act(sga, pg[:, 0:T], AF.Sigmoid)
                            if bi == 0:
                                K.tt(macc, py[:, 0:T], sga, mults)
                            else:
                                K.tt(tmp, py[:, 0:T], sga, mults)
                                K.tt(macc, macc, tmp, ALU.add)
                        K.cp(mT[:, n, 0:T], macc, eng="act")
                    wo = [load_w([("out", l, hf * 512, 512)]) for hf in range(2)]
                    ss, sd, rstd = col[0], col[1], col[2]
                    for i in range(NT):
                        r0, r1_ = tok0 + i * TT, tok0 + (i + 1) * TT
                        K.dma(xt[0:TT], xin[r0:r1_, :])
                        for hf in range(2):
                            po = nextpb()
                            for k in range(8):
                                K.mm(po[0:TT, 0:512], mT[:, k, i * TT:(i + 1) * TT], wo[hf][:, k, :], start=(k == 0), stop=(k == 7))
                            K.tt(xo[0:TT, hf * 512:(hf + 1) * 512], po[0:TT, 0:512], xt[0:TT, hf * 512:(hf + 1) * 512], ALU.add)
                        if not last:
                            K.dma(xout[r0:r1_, :], xo[0:TT])
                        else:
                            K.act(junk[0:TT], xo[0:TT], AF.Square, accum=ss[0:TT])
                            K.act(sd[0:TT], ss[0:TT], AF.Sqrt, scale=1.0 / D, bias=EPS)
                            K.recip(rstd[0:TT], sd[0:TT])
                            K.stt(xt[0:TT], xo[0:TT], rstd[0:TT], fngB[0:TT], mults, mults)
                            K.dma(yout[r0:r1_, :], xt[0:TT])
                    K.soft()

            phase_A()
            if c.stages < 3:
                return
            with contextlib.ExitStack() as e2:
                ubT = K.sb("ubT", [128, 8, SB], BF16, e2)
                phase_B()
                if c.stages < 4:
                    return
                ocT = K.sb("ocT", [128, 8, SB], BF16, e2)
                phase_C()
                if c.stages < 5:
                    return
                phase_M()

        convert_layer(0)
        for l in range(L):
            if l + 1 < L:
                convert_layer(l + 1)
            K.dma(gB, normg[l]); K.dma(caw, convaw[l]); K.dma(hp, hp8[l]); K.dma(ong, onormg[l])
            K.dma(cbw, convbw[l]); K.dma(bv, bvec[l])
            K.act(ea, hp[:, 0:8], AF.Exp)
            with contextlib.ExitStack() as esd:
                stg = [K.sb("stg%d" % i_, [128, 31, 128], BF16, esd) for i_ in range(2)]
                for ct_ in range(8):
                    for k_ in range(31):
                        K.act(stg[ct_ % 2][:, k_, :], identB, AF.Copy, scale=cbw[:, ct_, k_:k_ + 1])
                    K.dma(dgd[l][ct_], stg[ct_ % 2].v(lambda a: a.rearrange("p k m -> p (k m)")), par=True)
                K.soft()
            K.memset(Sst["p"][0], 0.0); K.memset(hista["p"], 0.0); K.memset(histb["p"], 0.0)
            K.memset(S16["p"][0], 0.0)
            for sbi in range(TP // SB):
                run_sb(l, "p", sbi)
            if c.stages >= 2:
                K.dma(cap[l], hista["p"][:, :, 0, :])
                K.dma(dlp[l].v(lambda a: a.rearrange("h d e -> d h e")), Sst["p"][0])
            if c.stages >= 3:
                K.dma(cbp[l], histb["p"][:, :, 0, :])
            for s_ in range(NSS):
                K.dma(hista["s"][:, :, s_, :], sca[l, s_])
                K.dma(Sst["s"][s_], sdl[l, s_].v(lambda a: a.rearrange("h d e -> d h e")))
                K.cp(S16["s"][s_], Sst["s"][s_], eng="pool")
                K.dma(histb["s"][:, :, s_, :], scb[l, s_])
            run_sb(l, "s", 0)
            for s_ in range(NSS):
                if c.stages >= 2:
                    K.dma(cas[l, s_], hista["s"][:, :, s_, :])
                    K.dma(dls[l, s_].v(lambda a: a.rearrange("h d e -> d h e")), Sst["s"][s_])
                if c.stages >= 3:
                    K.dma(cbs[l, s_], histb["s"][:, :, s_, :])
        K.barrier()
        K.finish()
    return nc, K, rec


def _prep(inp, cfg):
    c = cfg
    L, TP, NSS, TS, PAST = c.depth, c.tp, c.nss, c.ts, c.past
    f = lambda a: np.ascontiguousarray(np.asarray(a, dtype=np.float32))
    shared = {
        "w_in": f(inp["w_in"]), "w_oa": f(inp["w_o_a"]), "w_pb": f(inp["w_pw2_b"]), "w_oc": f(inp["w_o_c"]),
        "w_out": f(inp["w_out"]),
        "normg": f(np.broadcast_to(np.asarray(inp["norm_g"])[:, None, :], (L, 128, D))),
        "fng": f(np.broadcast_to(np.asarray(inp["final_norm_g"])[None, :], (128, D))),
        "convaw": f(np.asarray(inp["conv_a_w"]).reshape(L, 4, 24, 128).transpose(0, 3, 2, 1)),
        "hp8": f(np.broadcast_to(np.concatenate([np.asarray(inp["a_log"]), np.asarray(inp["dt_bias"])], 1)[:, None, :], (L, 128, 16))),
        "onormg": f(np.asarray(inp["onorm_a_g"]).reshape(L, 128, 1)),
        "convbw": f(np.asarray(inp["conv_b_w"]).reshape(L, 31, 8, 128).transpose(0, 3, 2, 1)),
        "bvec": f(np.stack([np.asarray(inp[k]).reshape(L, 8, 128).transpose(0, 2, 1) for k in ("conv_b_bias", "ln_b_g", "ln_b_b")], 2)),
        "cmisc": misc_consts(), "cchp": chunk_consts(128, 64), "cchs": chunk_consts(TS, TS),
    }
    fm, tm = rope_tables(np.arange(TP)); shared["rfm_p"], shared["rtm_p"] = fm, tm
    fm, tm = rope_tables(PAST + np.arange(TS))
    shared["rfm_s"] = np.ascontiguousarray(np.tile(fm, (1, 1, NSS))); shared["rtm_s"] = np.ascontiguousarray(np.tile(tm, (NSS, 1)))
    maps = []
    ncore = 8
    xpr = np.asarray(inp["x_prompt"]); xsm = np.asarray(inp["x_sample"])
    for core in range(ncore):
        m = dict(shared)
        m["xp"] = f(xpr[core // 2][:TP])
        sl = slice(core * NSS, (core + 1) * NSS)
        m["xs"] = f(xsm[sl].reshape(NSS * TS, D))
        m["ck"] = f(np.asarray(inp["cache_k"])[:, sl].reshape(L, NSS, PAST, 256))
        m["cv"] = f(np.asarray(inp["cache_v"])[:, sl].reshape(L, NSS, PAST, 256))
        m["cki"] = f(np.asarray(inp["cache_idx_k"])[:, sl])
        m["sca"] = f(np.asarray(inp["state_conv_a"])[:, sl].reshape(L, NSS, 3, 24, 128).transpose(0, 1, 4, 3, 2))
        m["sdl"] = f(np.asarray(inp["state_delta"])[:, sl])
        m["scb"] = f(np.asarray(inp["state_conv_b"])[:, sl].reshape(L, NSS, 30, 8, 128).transpose(0, 1, 4, 3, 2))
        maps.append(m)
    return maps


def _post(res, cfg):
    c = cfg
    L, TP, NSS, TS = c.depth, c.tp, c.nss, c.ts
    R = res.results
    ev = [R[i] for i in range(0, 8, 2)]
    cat = lambda k, rs: np.stack([r[k] for r in rs], 0)
    y_p = cat("yp", ev)
    y_s = np.concatenate([r["ys"].reshape(NSS, TS, D) for r in R], 0)
    nk_p = cat("nkp", ev).transpose(1, 0, 2, 3).reshape(L, 4, TP, 2, 128)
    nv_p = cat("nvp", ev).transpose(1, 0, 2, 3).reshape(L, 4, TP, 2, 128)
    nki_p = cat("nkip", ev).transpose(1, 0, 2, 3)
    ca_p = cat("cap", ev).transpose(1, 0, 4, 3, 2).reshape(L, 4, 3, 3072)
    dl_p = cat("dlp", ev).transpose(1, 0, 2, 3, 4)
    cb_p = cat("cbp", ev).transpose(1, 0, 4, 3, 2).reshape(L, 4, 30, 1024)
    nk_s = np.concatenate([r["nks"].reshape(L, NSS, TS, 2, 128) for r in R], 1)
    nv_s = np.concatenate([r["nvs"].reshape(L, NSS, TS, 2, 128) for r in R], 1)
    nki_s = np.concatenate([r["nkis"].reshape(L, NSS, TS, 64) for r in R], 1)
    ca_s = np.concatenate([r["cas"].transpose(0, 1, 4, 3, 2).reshape(L, NSS, 3, 3072) for r in R], 1)
    dl_s = np.concatenate([r["dls"] for r in R], 1)
    cb_s = np.concatenate([r["cbs"].transpose(0, 1, 4, 3, 2).reshape(L, NSS, 30, 1024) for r in R], 1)
    outs = (y_p, y_s, nk_p, nv_p, nki_p, ca_p, dl_p, cb_p, nk_s, nv_s, nki_s, ca_s, dl_s, cb_s)
    return tuple(np.ascontiguousarray(o, dtype=np.float32) for o in outs)


CFG = Cfg()


def kernel(**inputs):
    _, _, sched = build(CFG)
    nc, _, _ = build(CFG, sched)
    maps = _prep(inputs, CFG)
    res = run_bass_kernel_spmd(nc, maps, core_ids=list(range(8)))
    return _post(res, CFG)
```

```python
import contextlib
import numpy as np
import concourse.bass as bass
import concourse.mybir as mybir
from concourse.bass_utils import run_bass_kernel_spmd

F32 = mybir.dt.float32
BF16 = mybir.dt.bfloat16
AF = mybir.ActivationFunctionType
ALU = mybir.AluOpType
AX = mybir.AxisListType

D = 1024
NIN = 13400
EPS = 1e-6
O_QKV, O_A, O_B, O_ZA, O_GLU, O_ZB, O_QC, O_KC, O_VC, O_QI, O_KI, O_WI, O_ZC, O_GATE = (
    0, 3072, 3080, 3088, 4112, 6160, 7184, 8208, 8464, 8720, 9232, 9296, 9304, 10328)
NEG = -1.0e30


class Cfg:
    def __init__(self, depth=4, tp=4096, past=1024, ts=32, nss=2, sb=512, rounds=16):
        self.depth, self.tp, self.past, self.ts, self.nss, self.sb, self.rounds = depth, tp, past, ts, nss, sb, rounds
        self.lks = past + ts
        self.nsel_p = min(256, tp // 4)
        self.nsel_s = min(256, self.lks // 4)
        self.stages = 99


class Res:
    __slots__ = ("w", "rs", "excl")

    def __init__(self, excl=False, rs=None):
        self.w = {}
        self.rs = dict(rs) if rs else {}
        self.excl = excl


class TV:
    __slots__ = ("res", "ap")

    def __init__(self, res, ap):
        self.res = res
        self.ap = ap

    def __getitem__(self, idx):
        return TV(self.res, self.ap[idx])

    def v(self, fn):
        return TV(self.res, fn(self.ap))


class Eng:
    def __init__(self, h, sem):
        self.h = h
        self.sem = sem
        self.cnt = 0
        self.known = {}


class DQ:
    def __init__(self, sems):
        self.sems = sems
        self.i = 0
        self.vals = {s: 0 for s in sems}


def _ap(x):
    return x.ap if isinstance(x, TV) else x


class KB:
    def __init__(self, nc, es, cfg):
        self.nc, self.es, self.c = nc, es, cfg
        self.E = {}
        for name, h in (("pe", nc.tensor), ("act", nc.scalar), ("dve", nc.vector), ("pool", nc.gpsimd), ("sp", nc.sync)):
            self.E[name] = Eng(h, es.enter_context(nc.semaphore("s_" + name)))
        self.dq = {}
        for name in ("sp", "pool", "act"):
            self.dq[name] = DQ([es.enter_context(nc.semaphore("d_%s%d" % (name, i))) for i in range(8)])
        self.n_ins = 0

    def sb(self, name, shape, dt=F32, es=None):
        self.uid = getattr(self, "uid", 0) + 1
        name = "%s_%d" % (name, self.uid)
        t = (es or self.es).enter_context(self.nc.sbuf_tensor(name, list(shape), dt))
        return TV(Res(rs=getattr(self, "join", None)), t[:])

    def ps(self, name, shape, dt=F32):
        t = self.es.enter_context(self.nc.psum_tensor(name, list(shape), dt))
        return TV(Res(excl=True), t[:])

    def dram(self, name, shape, kind, dt=F32):
        return TV(Res(), self.nc.dram_tensor(name, list(shape), dt, kind=kind).ap())

    def _emit(self, en, fn, reads, writes, dma=False, par=False):
        E = self.E[en]
        need = {}

        def add1(s, v):
            if en == "pe" and s is E.sem:
                return
            if need.get(s, 0) < v:
                need[s] = v

        def add(evs):
            for s, v in evs.items():
                add1(s, v)

        xr = [r for r in reads if r.res.excl]
        if xr:
            reads = [r for r in reads if not r.res.excl]
            writes = list(writes) + xr
        for r in reads:
            add(r.res.w)
        for w in writes:
            if not par:
                add(w.res.w)
            add(w.res.rs)
        for s, v in need.items():
            if E.known.get(s, 0) < v:
                E.h.wait_ge(s, v)
                E.known[s] = v
        ins = fn(E.h)
        self.n_ins += 1
        if dma:
            q = self.dq[en]
            sem = q.sems[q.i % len(q.sems)]
            q.i += 1
            q.vals[sem] += 16
            ev = (sem, q.vals[sem])
            ins.then_inc(sem, 16)
        else:
            E.cnt += 1
            ev = (E.sem, E.cnt)
            ins.then_inc(E.sem, 1)
        for r in reads:
            rs = r.res.rs
            if rs.get(ev[0], 0) < ev[1]:
                rs[ev[0]] = ev[1]
        for w in writes:
            if par:
                if w.res.w.get(ev[0], 0) < ev[1]:
                    w.res.w[ev[0]] = ev[1]
            else:
                w.res.w = {ev[0]: ev[1]}
                w.res.rs = {}

    def soft(self):
        j = {}
        for E in self.E.values():
            if E.cnt:
                j[E.sem] = E.cnt
        for q in self.dq.values():
            for s, v in q.vals.items():
                if v:
                    j[s] = v
        self.join = j

    def barrier(self):
        evs = [(E.sem, E.cnt) for E in self.E.values() if E.cnt]
        for q in self.dq.values():
            evs += [(s, v) for s, v in q.vals.items() if v]
        for E in self.E.values():
            for s, v in evs:
                if s is E.sem:
                    continue
                if E.known.get(s, 0) < v:
                    E.h.wait_ge(s, v)
                    E.known[s] = v

    def finish(self):
        E = self.E["sp"]
        for q in self.dq.values():
            for s, v in q.vals.items():
                if v and E.known.get(s, 0) < v:
                    E.h.wait_ge(s, v)
        for name, X in self.E.items():
            if name != "sp" and X.cnt:
                E.h.wait_ge(X.sem, X.cnt)

    def mm(self, out, lhsT, rhs, start=True, stop=True, **kw):
        self._emit("pe", lambda e: e.matmul(out.ap, lhsT=lhsT.ap, rhs=rhs.ap, start=start, stop=stop, **kw),
                   [lhsT, rhs], [out])

    def tr(self, out, in_, ident):
        self._emit("pe", lambda e: e.transpose(out.ap, in_.ap, ident.ap), [in_, ident], [out])

    def act(self, out, in_, func, bias=None, scale=None, accum=None):
        reads = [in_] + [x for x in (bias, scale) if isinstance(x, TV)]
        writes = [out] + ([accum] if accum is not None else [])
        kw = {}
        if bias is not None:
            kw["bias"] = _ap(bias)
        if scale is not None:
            kw["scale"] = _ap(scale)
        if accum is not None:
            kw["accum_out"] = accum.ap
        self._emit("act", lambda e: e.activation(out.ap, in_.ap, func, **kw), reads, writes)

    def ts(self, out, in0, s1, s2=None, op0=ALU.mult, op1=None, accum=None, eng="dve"):
        reads = [in0] + [x for x in (s1, s2) if isinstance(x, TV)]
        writes = [out] + ([accum] if accum is not None else [])
        kw = {}
        if op1 is not None:
            kw["op1"] = op1
        if accum is not None:
            kw["accum_out"] = accum.ap
        self._emit(eng, lambda e: e.tensor_scalar(out.ap, in0.ap, _ap(s1), _ap(s2), op0, **kw), reads, writes)

    def tt(self, out, in0, in1, op, eng="dve"):
        self._emit(eng, lambda e: e.tensor_tensor(out.ap, in0.ap, in1.ap, op), [in0, in1], [out])

    def stt(self, out, in0, scalar, in1, op0, op1):
        reads = [in0, in1] + ([scalar] if isinstance(scalar, TV) else [])
        self._emit("dve", lambda e: e.scalar_tensor_tensor(out.ap, in0.ap, _ap(scalar), in1.ap, op0, op1), reads, [out])

    def cp(self, out, in_, eng="dve"):
        if eng == "act":
            self._emit("act", lambda e: e.copy(out.ap, in_.ap), [in_], [out])
        else:
            self._emit(eng, lambda e: e.tensor_copy(out.ap, in_.ap), [in_], [out])

    def memset(self, out, val, eng="dve"):
        self._emit(eng, lambda e: e.memset(out.ap, val), [], [out])

    def recip(self, out, in_):
        self._emit("dve", lambda e: e.reciprocal(out.ap, in_.ap), [in_], [out])

    def reduce(self, out, in_, op, absval=False):
        self._emit("dve", lambda e: e.tensor_reduce(out.ap, in_.ap, AX.X, op, apply_absolute_value=absval), [in_], [out])

    def dma(self, out, in_, q="pool", par=False):
        self._emit(q, lambda e: e.dma_start(out=out.ap, in_=in_.ap), [in_], [out], dma=True, par=par)


def chunk_consts(tt, c):
    i = np.arange(tt)
    same = (i[:, None] // c) == (i[None, :] // c)
    out = np.zeros((128, 5, 128), np.float32)
    out[:tt, 0, :tt] = ((i[:, None] > i[None, :]) & same)
    out[:tt, 1, :tt] = ((i[:, None] <= i[None, :]) & same)
    out[:tt, 2, :tt] = (i[:, None] == (i[None, :] // c) * c + c - 1)
    nch = tt // c
    for k in range(nch):
        out[:tt, 3, k] = (i // c == k)
        out[:tt, 3, 8 + k] = (i == k * c + c - 1)
    return out


def rope_tables(pos):
    pos = np.asarray(pos, np.float64)
    t = len(pos)
    inv128 = 500000.0 ** (-np.arange(16) * 2.0 / 32)
    inv64 = 500000.0 ** (-np.arange(8) * 2.0 / 16)
    a128 = pos[:, None] * inv128[None, :]
    a64 = pos[:, None] * inv64[None, :]
    c128 = np.ones((128, t)); s128 = np.zeros((128, t))
    c128[0:16] = np.cos(a128).T; c128[16:32] = np.cos(a128).T
    s128[0:16] = -np.sin(a128).T; s128[16:32] = np.sin(a128).T
    c128 *= 128 ** -0.5; s128 *= 128 ** -0.5
    c64 = np.ones((128, t)); s64 = np.zeros((128, t))
    for b in (0, 64):
        c64[b:b + 8] = np.cos(a64).T; c64[b + 8:b + 16] = np.cos(a64).T
        s64[b:b + 8] = -np.sin(a64).T; s64[b + 8:b + 16] = np.sin(a64).T
    c64 *= 64 ** -0.5; s64 *= 64 ** -0.5
    fm = np.stack([c128, s128, c64, s64], 1).astype(np.float32)
    tk = np.zeros((t, 2, 2, 16)); tk[:, 0] = np.cos(a128)[:, None, :]; tk[:, 1] = np.sin(a128)[:, None, :]
    tki = np.zeros((t, 2, 8)); tki[:, 0] = np.cos(a64); tki[:, 1] = np.sin(a64)
    tm = np.concatenate([tk.reshape(t, 64), tki.reshape(t, 16)], 1).astype(np.float32)
    return fm, tm


def misc_consts():
    m = np.zeros((128, 8, 128), np.float32)
    m[:, 0] = np.eye(128)
    m[:, 1] = 1.0
    for mm_ in range(16):
        m[mm_ + 16, 2, mm_] = 1.0
        m[mm_, 2, mm_ + 16] = 1.0
    for b in (0, 64):
        for mm_ in range(8):
            m[b + mm_ + 8, 3, b + mm_] = 1.0
            m[b + mm_, 3, b + mm_ + 8] = 1.0
    r = np.arange(128)
    adm = (r[None, :] < 64) | (r[:, None] >= 64)
    m[:, 4] = adm
    m[:, 5] = np.where(adm, 0.0, NEG)
    return m


def build(cfg, schedule=None):
    c = cfg
    rec = []
    nc = bass.Bass("TRN2", target_bir_lowering=False)
    es = contextlib.ExitStack()
    with es:
        K = KB(nc, es, c)
        L, TP, NSS, TS, PAST = c.depth, c.tp, c.nss, c.ts, c.past
        TSS = NSS * TS
        NKB_P = TP // 128
        di = lambda n, s: K.dram(n, s, "ExternalInput")
        do = lambda n, s: K.dram(n, s, "ExternalOutput")
        xp = di("xp", [TP, D]); xs = di("xs", [TSS, D])
        ck = di("ck", [L, NSS, PAST, 256]); cv = di("cv", [L, NSS, PAST, 256]); cki = di("cki", [L, NSS, PAST, 64])
        sca = di("sca", [L, NSS, 128, 24, 3]); sdl = di("sdl", [L, NSS, 8, 128, 128]); scb = di("scb", [L, NSS, 128, 8, 30])
        w_in = di("w_in", [L, D, NIN])
        w_oa = di("w_oa", [L, D, D]); w_pb = di("w_pb", [L, D, D]); w_oc = di("w_oc", [L, D, D]); w_out = di("w_out", [L, D, D])
        normg = di("normg", [L, 128, D]); fng = di("fng", [128, D])
        convaw = di("convaw", [L, 128, 24, 4]); hp8 = di("hp8", [L, 128, 16]); onormg = di("onormg", [L, 128, 1])
        convbw = di("convbw", [L, 128, 8, 31]); bvec = di("bvec", [L, 128, 3, 8])
        cmisc = di("cmisc", [128, 8, 128]); cchp = di("cchp", [128, 5, 128]); cchs = di("cchs", [128, 5, 128])
        rfm_p = di("rfm_p", [128, 4, TP]); rtm_p = di("rtm_p", [TP, 80])
        rfm_s = di("rfm_s", [128, 4, TSS]); rtm_s = di("rtm_s", [TSS, 80])
        yp = do("yp", [TP, D]); ys = do("ys", [TSS, D])
        nkp = do("nkp", [L, TP, 256]); nvp = do("nvp", [L, TP, 256]); nkip = do("nkip", [L, TP, 64])
        cap = do("cap", [L, 128, 24, 3]); dlp = do("dlp", [L, 8, 128, 128]); cbp = do("cbp", [L, 128, 8, 30])
        nks = do("nks", [L, TSS, 256]); nvs = do("nvs", [L, TSS, 256]); nkis = do("nkis", [L, TSS, 64])
        cas = do("cas", [L, NSS, 128, 24, 3]); dls = do("dls", [L, NSS, 8, 128, 128]); cbs = do("cbs", [L, NSS, 128, 8, 30])
        xscp = [K.dram("xscp%d" % i, [TP, D], "Internal") for i in range(2)]
        xscs = [K.dram("xscs%d" % i, [TSS, D], "Internal") for i in range(2)]

        SB = c.sb
        cm = K.sb("cm", [128, 8, 128]); K.dma(cm, cmisc)
        cmb = K.sb("cmb", [128, 4, 128], BF16); K.dma(cmb, cmisc[:, 0:4, :], q="pool")
        identF, onesF = cm[:, 0, :], cm[:, 1, :]
        identB, onesB, p128T, p64T = cmb[:, 0, :], cmb[:, 1, :], cmb[:, 2, :], cmb[:, 3, :]
        adm01, admneg = cm[:, 4, :], cm[:, 5, :]
        cch = {"p": K.sb("cchp_s", [128, 5, 128]), "s": K.sb("cchs_s", [128, 5, 128])}
        K.dma(cch["p"], cchp); K.dma(cch["s"], cchs)
        hTs = [K.sb("hT", [128, 8, SB], BF16), K.sb("hT_b", [128, 8, SB], BF16)]
        cur = {"hT": hTs[0]}
        NW = 4
        DEPTH = 2
        wb = [K.sb("wb%d" % i, [128, 8, 512], BF16) for i in range(NW)]
        wbf = {"in": [K.dram("wbf_in%d" % l_, [D, NIN], "Internal", BF16) for l_ in range(L)]}
        for nm in ("oa", "pb", "oc", "out"):
            wbf[nm] = [K.dram("wbf_%s%d" % (nm, l_), [D, D], "Internal", BF16) for l_ in range(L)]
        wf32 = {"in": w_in, "oa": w_oa, "pb": w_pb, "oc": w_oc, "out": w_out}
        dgd = [K.dram("dgd%d" % l_, [8, 128, 3968], "Internal", BF16) for l_ in range(L)]

        def convert_layer(l_):
            for nm, ncols in (("in", NIN), ("oa", D), ("pb", D), ("oc", D), ("out", D)):
                cstep = 3350 if nm == "in" else 1024
                for r0 in range(0, D, 128):
                    for c0 in range(0, ncols, cstep):
                        K.dma(wbf[nm][l_][r0:r0 + 128, c0:c0 + cstep], wf32[nm][l_][r0:r0 + 128, c0:c0 + cstep], q="pool", par=True)
        ogT = K.sb("ogT", [128, 8, SB], BF16)
        LKS = c.lks
        KTW = max(TP, NSS * LKS)
        kT = K.sb("kT", [128, 2, KTW], BF16)
        NVB = max(NKB_P, NSS * 9)
        vaug = K.sb("vaug", [128, NVB, 2, 130], BF16)
        kiT2 = K.sb("kiT2", [128, KTW], BF16)
        Sst = {"p": [K.sb("S_p", [128, 8, 128])]}
        S16 = {"p": [K.sb("S16_p", [128, 8, 128], BF16)]}
        hista = {"p": K.sb("hista_p", [128, 24, 1, 3])}
        histb = {"p": K.sb("histb_p", [128, 8, 1, 30])}
        gB = K.sb("gB", [128, D]); caw = K.sb("caw", [128, 24, 4]); hp = K.sb("hp", [128, 16]); ea = K.sb("ea", [128, 8])
        ong = K.sb("ong", [128, 1]); cbw = K.sb("cbw", [128, 8, 31]); bv = K.sb("bv", [128, 3, 8])
        rtm = K.sb("rtm", [128, 4, 80])
        wtok = K.sb("wtok", [128, 4, 8])
        GT = K.sb("gat", [128, 4, 12, 8])
        geB = K.sb("geB", [128, 4, 2, 8])
        sm = [K.sb("sm%d" % i, [128, 8]) for i in range(8)]
        col = [K.sb("col%d" % i, [128, 1]) for i in range(12)]
        PB = [K.ps("pb%d" % i, [128, 512]) for i in range(7)]
        TRB = K.ps("trb", [128, 1024], BF16)
        pbi = [0]

        def nextpb(lo=0, hi=7):
            pbi[0] = (pbi[0] + 1) % (hi - lo)
            return PB[lo + pbi[0]]

        K.memset(vaug[:, :, :, 128:130], 1.0)
        wbi = [0]

        issued = [0]

        def issue_w(k, pieces):
            dst = wb[k % NW]
            off = 0
            for (nm, l_, c0, n) in pieces:
                if nm == "dg":
                    K.dma(dst.v(lambda a: a.rearrange("p k n -> p (k n)"))[:, 0:3968], dgd[l_][c0], q="sp")
                    continue
                K.dma(dst[:, :, off:off + n], wbf[nm][l_][:, c0:c0 + n].v(lambda a: a.rearrange("(k p) n -> p k n", p=128)), q="sp")
                off += n

        def load_w(pieces):
            k = len(rec)
            rec.append(list(pieces))
            if schedule is None:
                issue_w(k, pieces)
            else:
                assert schedule[k] == list(pieces), (k, schedule[k], pieces)
                while issued[0] < min(len(schedule), k + DEPTH + 1):
                    issue_w(issued[0], schedule[issued[0]])
                    issued[0] += 1
            return wb[k % NW]

        def proj_fm(ps, w, c0, ncol, T, rhs=None):
            src = cur["hT"] if rhs is None else rhs
            for k in range(8):
                K.mm(ps[0:ncol, 0:T], w[:, k, c0:c0 + ncol], src[:, k, 0:T], start=(k == 0), stop=(k == 7))

        def run_sb(l, kind, sbi, pro_done=False, overlap_next=False):
            if kind == "p":
                nseq, tl, C, T, TT, tok0 = 1, SB, 64, SB, 128, sbi * SB
                xin = xp if l == 0 else xscp[(l - 1) % 2]
                xout = xscp[l % 2]; yout = yp
                nko, nvo, nkio = nkp, nvp, nkip
                rfd, rtd = rfm_p, rtm_p
                nsel = c.nsel_p
            else:
                nseq, tl, C, T, TT, tok0 = NSS, TS, TS, TSS, TS, 0
                xin = xs if l == 0 else xscs[(l - 1) % 2]
                xout = xscs[l % 2]; yout = ys
                nko, nvo, nkio = nks, nvs, nkis
                rfd, rtd = rfm_s, rtm_s
                nsel = c.nsel_s
            NT = T // TT
            nch = TT // C
            cc = cch[kind]
            Mstrict, Mupper, SelEnd = cc[:, 0, :], cc[:, 1, :], cc[:, 2, :]
            cmk = cc[:, 3, :]
            last = (l == L - 1)
            wl = w_in[l]

            def wcols(c0, n):
                return ("in", l, c0, n)

            def lock(gens):
                gens = list(gens)
                while gens:
                    for g_ in list(gens):
                        try:
                            next(g_)
                        except StopIteration:
                            gens.remove(g_)

            def make_prologue(esx, tok0, hT):
                xts = [K.sb("xt", [128, D], F32, esx) for _ in range(2)]
                xns = [K.sb("xn", [128, D], BF16, esx) for _ in range(NT)]
                c3 = K.sb("c3", [128, NT, 3], F32, esx)

                def p0_gen(i):
                    xt, junk, xn = xts[i % 2], xns[i], xns[i]
                    ss, sd, rstd = c3[:, i, 0:1], c3[:, i, 1:2], c3[:, i, 2:3]
                    K.dma(xt[0:TT], xin[tok0 + i * TT: tok0 + (i + 1) * TT, :])
                    yield
                    K.act(junk[0:TT], xt[0:TT], AF.Square, accum=ss[0:TT])
                    yield
                    K.act(sd[0:TT], ss[0:TT], AF.Sqrt, scale=1.0 / D, bias=EPS)
                    yield
                    K.recip(rstd[0:TT], sd[0:TT])
                    yield
                    K.stt(xn[0:TT], xt[0:TT], rstd[0:TT], gB[0:TT], ALU.mult, ALU.mult)
                    yield
                    for f in range(8):
                        K.tr(TRB[:, f * 128: f * 128 + TT], xn[0:TT, f * 128:(f + 1) * 128], identB[0:TT, 0:TT])
                    K.cp(hT[:, :, i * TT:(i + 1) * TT],
                         TRB.v(lambda a: a.rearrange("p (f t) -> p f t", f=8))[:, :, 0:TT], eng=("act" if i % 2 else "dve"))
                    yield

                wkv = K.sb("wkv", [128, 8, 512], BF16, esx)
                wsm = K.sb("wsm", [128, 8, 88], BF16, esx)
                RW = lambda a: a.rearrange("(k p) n -> p k n", p=128)
                K.dma(wkv, wbf["in"][l][:, O_KC:O_KC + 512].v(RW), q="sp")
                K.dma(wsm[:, :, 0:16], wbf["in"][l][:, O_A:O_A + 16].v(RW), q="sp")
                K.dma(wsm[:, :, 16:88], wbf["in"][l][:, O_KI:O_KI + 72].v(RW), q="sp")
                for i in range(NT):
                    K.dma(rtm[0:TT, i, :], rtd[tok0 + i * TT: tok0 + (i + 1) * TT, :])
                mk1 = lambda n, shp, dt=F32: K.sb(n, shp, dt, esx)

                def kv_gen(i):
                    kf = mk1("kf", [128, 256]); vf = mk1("vf", [128, 256]); kif = mk1("kif", [128, 64])
                    kb16 = mk1("kb16", [128, 256], BF16); ki2b = mk1("ki2b", [128, 128], BF16)
                    t1 = mk1("t1", [128, 2, 16]); t2 = mk1("t2", [128, 2, 16]); t3 = mk1("t3", [128, 2, 16]); t4 = mk1("t4", [128, 2, 16])
                    smi = [mk1("smi", [128, 8]) for _ in range(2)]
                    pkv = PB[i]
                    bB = PB[4]
                    o_ = i * 128
                    psm = bB[:, o_:o_ + 88]
                    pg = bB[:, o_ + 88:o_ + 104]
                    pge = bB[:, o_ + 104:o_ + 120]
                    tsl = slice(i * TT, (i + 1) * TT)
                    for k in range(8):
                        K.mm(pkv[0:TT, 0:512], hT[:, k, tsl], wkv[:, k, :], start=(k == 0), stop=(k == 7))
                    for k in range(8):
                        K.mm(psm[0:TT, 0:88], hT[:, k, tsl], wsm[:, k, 0:88], start=(k == 0), stop=(k == 7))
                    yield
                    k3 = pkv[0:TT, 0:256].v(lambda a: a.rearrange("p (h d) -> p h d", h=2))
                    kf3 = kf[0:TT].v(lambda a: a.rearrange("p (h d) -> p h d", h=2))
                    tb = rtm[0:TT, i, 0:64].v(lambda a: a.rearrange("p (a h d) -> p a h d", a=2, h=2))
                    cs, sn = tb[:, 0], tb[:, 1]
                    x1, x2 = k3[:, :, 0:16], k3[:, :, 16:32]
                    K.cp(kf[0:TT], pkv[0:TT, 0:256], eng="act")
                    K.tt(t1[0:TT], x1, cs, ALU.mult); K.tt(t2[0:TT], x2, sn, ALU.mult)
                    K.tt(t3[0:TT], x2, cs, ALU.mult); K.tt(t4[0:TT], x1, sn, ALU.mult)
                    K.cp(vf[0:TT], pkv[0:TT, 256:512], eng="act")
                    g_ = lambda j: GT[0:TT, i, j, :]
                    xa, ax, e1, l1, spl, gg, beta, Gs, nG, eG, kds, eGe = [g_(j) for j in range(12)]
                    K.tt(xa, psm[0:TT, 0:8], hp[0:TT, 8:16], ALU.add)
                    K.act(beta, psm[0:TT, 8:16], AF.Sigmoid)
                    K.act(wtok[0:TT, i, :], psm[0:TT, 80:88], AF.Copy, scale=float(8 ** -0.5))
                    yield
                    K.tt(kf3[:, :, 0:16], t1[0:TT], t2[0:TT], ALU.subtract)
                    K.tt(kf3[:, :, 16:32], t3[0:TT], t4[0:TT], ALU.add)
                    K.dma(nvo[l, tok0 + i * TT: tok0 + (i + 1) * TT, :], vf[0:TT])
                    K.act(ax, xa, AF.Abs)
                    tbi = rtm[0:TT, i, 64:80].v(lambda a: a.rearrange("p (a d) -> p a d", a=2))
                    csi, sni = tbi[:, 0], tbi[:, 1]
                    kip = psm[0:TT, 16:80]
                    K.cp(kif[0:TT], kip, eng="act")
                    yield
                    K.dma(nko[l, tok0 + i * TT: tok0 + (i + 1) * TT, :], kf[0:TT])
                    K.cp(kb16[0:TT], kf[0:TT], eng="pool")
                    u1, u2, u3, u4 = t1[0:TT, 0, 0:8], t2[0:TT, 0, 0:8], t3[0:TT, 0, 0:8], t4[0:TT, 0, 0:8]
                    K.tt(u1, kip[:, 0:8], csi, ALU.mult); K.tt(u2, kip[:, 8:16], sni, ALU.mult)
                    K.tt(u3, kip[:, 8:16], csi, ALU.mult); K.tt(u4, kip[:, 0:8], sni, ALU.mult)
                    K.act(e1, ax, AF.Exp, scale=-1.0)
                    yield
                    K.tt(kif[0:TT, 0:8], u1, u2, ALU.subtract)
                    K.tt(kif[0:TT, 8:16], u3, u4, ALU.add)
                    K.act(l1, e1, AF.Ln, bias=1.0)
                    if kind == "p":
                        kcol = tok0 + i * TT
                        vblk = (tok0 + i * TT) // 128
                    else:
                        kcol = i * LKS + PAST
                        vblk = i * 9 + 8
                    K.cp(vaug[0:TT, vblk, :, 0:128], vf[0:TT].v(lambda a: a.rearrange("p (h d) -> p h d", h=2)), eng="pool")
                    yield
                    for h in range(2):
                        K.tr(TRB[:, h * 128: h * 128 + TT], kb16[0:TT, h * 128:(h + 1) * 128], identB[0:TT, 0:TT])
                    K.cp(kT[:, :, kcol:kcol + TT], TRB[:, 0:256].v(lambda a: a.rearrange("p (h t) -> p h t", h=2))[:, :, 0:TT], eng="act")
                    K.dma(nkio[l, tok0 + i * TT: tok0 + (i + 1) * TT, :], kif[0:TT])
                    K.cp(ki2b[0:TT, 0:64], kif[0:TT], eng="pool"); K.cp(ki2b[0:TT, 64:128], kif[0:TT], eng="pool")
                    K.stt(spl, xa, 0.0, l1, ALU.max, ALU.add)
                    yield
                    K.stt(gg, spl, -1.0, ea[0:TT], ALU.mult, ALU.mult)
                    yield
                    K.tr(TRB[:, 256: 256 + TT], ki2b[0:TT, :], identB[0:TT, 0:TT])
                    K.cp(kiT2[:, kcol:kcol + TT], TRB[:, 256:256 + TT], eng="dve")
                    K.mm(pg[0:TT, 0:8], Mupper[0:TT, 0:TT], gg)
                    yield
                    K.cp(Gs, pg[0:TT, 0:8])
                    yield
                    K.mm(pg[0:TT, 8:16], SelEnd[0:TT, 0:TT], Gs)
                    K.act(nG, Gs, AF.Copy, scale=-1.0)
                    K.act(eG, Gs, AF.Exp)
                    yield
                    K.tt(kds, pg[0:TT, 8:16], Gs, ALU.subtract)
                    K.act(eGe, pg[0:TT, 8:16], AF.Exp)
                    K.tt(xa, beta, eG, ALU.mult)
                    K.act(ax, beta, AF.Copy, scale=-1.0)
                    yield
                    K.act(kds, kds, AF.Exp)
                    for ch in range(nch):
                        K.ts(smi[ch][0:TT], eGe, cmk[0:TT, 8 + ch: 9 + ch], None, op0=ALU.mult)
                    yield
                    for ch in range(nch):
                        K.mm(pge[:, ch * 8:(ch + 1) * 8], onesF[0:TT, :], smi[ch][0:TT])
                    for ch in range(nch):
                        K.ts(GT[0:TT, i, 2 + ch, :], eG, cmk[0:TT, ch:ch + 1], None, op0=ALU.mult)
                        K.ts(GT[0:TT, i, 4 + ch, :], kds, cmk[0:TT, ch:ch + 1], None, op0=ALU.mult)
                    yield
                    K.cp(geB[:, i, 0:nch, :], pge[:, 0:nch * 8].v(lambda a: a.rearrange("p (c h) -> p c h", c=nch)))
                    yield

                def both(i):
                    yield from p0_gen(i)
                    yield from kv_gen(i)

                def seq(idx):
                    for i_ in idx:
                        yield from both(i_)

                return [seq(range(par, NT, 2)) for par in range(min(2, NT))]

            hT = hTs[sbi % 2]
            cur["hT"] = hT
            if not pro_done:
                with contextlib.ExitStack() as esp:
                    lock(make_prologue(esp, tok0, hT))
                    K.soft()
            RE = lambda pat, **kw: (lambda a: a.rearrange(pat, **kw))
            ubT = ocT = None
            mults = ALU.mult

            def lock(gens):
                gens = list(gens)
                while gens:
                    for g_ in list(gens):
                        try:
                            next(g_)
                        except StopIteration:
                            gens.remove(g_)

            def chain(*gs):
                for g_ in gs:
                    yield from g_

            def lockgen(gens):
                gens = list(gens)
                while gens:
                    for g_ in list(gens):
                        try:
                            next(g_)
                        except StopIteration:
                            gens.remove(g_)
                    yield

            def phase_A():
                with contextlib.ExitStack() as ea_:
                    mk = lambda n, shp, dt=F32: K.sb(n, shp, dt, ea_)
                    qkvb = [[mk("qkv%d_%d" % (j, par), [128, T]) for j in range(3)] for par in range(2)]
                    qk16 = [[mk("qk16%d_%d" % (j, par), [128, T], BF16) for j in range(2)] for par in range(2)]
                    zgb = [mk("zg%d" % par, [128, T], BF16) for par in range(2)]
                    xc = mk("xc", [128, nseq, 3 + tl]); acc = mk("acc", [128, nseq, tl]); sl = mk("sl", [128, T])
                    sq = mk("sq", [128, T], BF16); sdd = mk("sdd", [128, T])
                    junkA = mk("junkA", [128, 128], BF16)
                    pre_f32 = ("t0", "E", "Ls", "Eu")
                    pre_b16 = ("vb", "kbg", "LF", "Fb", "Ma", "MTa", "Mb", "MTb", "Xa", "Xb")
                    pre = []
                    for i_ in range(NT):
                        dct = dict((n, mk("%s_%d" % (n, i_), [128, 128])) for n in pre_f32)
                        dct.update((n, mk("%s_%d" % (n, i_), [128, 128], BF16)) for n in pre_b16)
                        pre.append(dct)
                    hand = [[dict(u=mk("u", [128, 128]), wT=mk("wT", [128, 128], BF16), kd2=mk("kd2", [128, 2, 128], BF16),
                                  qk2=mk("qk2", [128, 2, 128], BF16))
                             for i_ in range(NT)] for par in range(2)]
                    scl = [dict(vn0=mk("vn0", [128, 128], BF16), vn1=mk("vn1", [128, 128], BF16),
                                oacc=mk("oacc", [128, 128]), onb=mk("onb", [128, 128], BF16),
                                c0=mk("c0", [128, 1]), c1=mk("c1", [128, 1]), c2=mk("c2", [128, 1])) for par in range(2)]
                    nlev = {64: 5, 32: 4}[C]

                    def pre_gen(h, i):
                        S = pre[i]
                        H = hand[h % 2][i]
                        qkv = qkvb[h % 2]
                        tsl = slice(i * TT, (i + 1) * TT)
                        g = lambda j: GT[0:TT, i, j, h:h + 1]
                        bexpG, negb, beta, Gc, nGc = g(0), g(1), g(6), g(7), g(8)
                        kdsc = [g(4 + ch) for ch in range(nch)]
                        iF = identF[0:TT, 0:TT]
                        sq_ = lambda t: t[0:TT, 0:TT]
                        p1 = PB[i]
                        K.tr(p1[0:TT, 0:128], qkv[2][:, tsl], identF)
                        K.tr(p1[0:TT, 128:256], qkv[1][:, tsl], identF)
                        q16, k16 = qk16[h % 2]
                        K.mm(p1[0:TT, 256:256 + TT], k16[:, tsl], k16[:, tsl])
                        K.mm(p1[0:TT, 384:384 + TT], k16[:, tsl], q16[:, tsl])
                        K.act(sq_(S["t0"]), iF, AF.Copy, scale=nGc)
                        yield
                        K.ts(S["vb"][0:TT], p1[0:TT, 0:128], beta, None, op0=mults)
                        K.ts(S["kbg"][0:TT], p1[0:TT, 128:256], bexpG, None, op0=mults)
                        for ch in range(nch):
                            K.act(H["kd2"][0:TT, ch, :], p1[0:TT, 128:256], AF.Copy, scale=kdsc[ch])
                        p2 = PB[i]
                        K.mm(p2[0:TT, 0:TT], onesF[0:TT, 0:TT], sq_(S["t0"]))
                        yield
                        K.act(sq_(S["t0"]), p2[0:TT, 0:TT], AF.Abs, bias=Gc)
                        yield
                        K.act(sq_(S["E"]), sq_(S["t0"]), AF.Exp, scale=-1.0)
                        yield
                        K.tt(sq_(S["Ls"]), sq_(S["E"]), Mstrict[0:TT, 0:TT], mults)
                        K.tt(sq_(S["Eu"]), sq_(S["E"]), Mupper[0:TT, 0:TT], mults, eng="pool")
                        yield
                        K.stt(sq_(S["E"]), p1[0:TT, 256:256 + TT], negb, sq_(S["Ls"]), mults, mults)
                        K.tt(sq_(S["t0"]), p1[0:TT, 384:384 + TT], sq_(S["Eu"]), mults)
                        yield
                        for ch in range(nch):
                            K.act(H["qk2"][0:TT, ch, 0:TT], sq_(S["t0"]), AF.Copy, scale=cmk[0:TT, ch:ch + 1])
                        p3 = PB[i][:, 128:256]
                        K.tr(p3[0:TT, 0:TT], sq_(S["E"]), iF)
                        K.cp(sq_(S["Ma"]), sq_(S["E"]), eng="pool")
                        yield
                        K.cp(sq_(S["MTa"]), p3[0:TT, 0:TT], eng="act")
                        K.tt(sq_(S["Xa"]), p3[0:TT, 0:TT], iF, ALU.add)
                        yield
                        M, MT, Mn, MTn, X, Xn = S["Ma"], S["MTa"], S["Mb"], S["MTb"], S["Xa"], S["Xb"]
                        pk = PB[i]
                        iB = identB[0:TT, 0:TT]
                        for kl in range(1, nlev + 2):
                            do_sq = kl <= nlev
                            lastl = (kl == nlev)
                            if kl > 1:
                                K.mm(pk[0:TT, 256:256 + TT], iB, sq_(X), start=True, stop=False)
                                K.mm(pk[0:TT, 256:256 + TT], sq_(M), sq_(X), start=False, stop=True)
                            if do_sq and not (lastl and False):
                                K.mm(pk[0:TT, 0:TT], sq_(MT), sq_(M))
                                if not lastl:
                                    K.mm(pk[0:TT, 128:128 + TT], sq_(M), sq_(MT))
                            yield
                            if kl > 1:
                                K.cp(sq_(Xn), pk[0:TT, 256:256 + TT])
                                X, Xn = Xn, X
                            if do_sq:
                                K.cp(sq_(Mn), pk[0:TT, 0:TT], eng="act")
                                if not lastl:
                                    K.cp(sq_(MTn), pk[0:TT, 128:128 + TT], eng="act")
                                M, Mn = Mn, M
                                MT, MTn = MTn, MT
                            yield
                        p4 = PB[i]
                        K.mm(p4[0:TT, 0:128], sq_(X), S["vb"][0:TT])
                        K.mm(p4[:, 128:128 + TT], S["kbg"][0:TT], sq_(X))
                        yield
                        K.cp(H["u"][0:TT], p4[0:TT, 0:128], eng="act")
                        K.cp(H["wT"][:, 0:TT], p4[:, 128:128 + TT], eng="act")
                        yield

                    def scan_gen(h):
                        qkv = qkvb[h % 2]
                        zg = zgb[h % 2]
                        for i in range(NT):
                            H = dict(hand[h % 2][i])
                            H.update(scl[i % 2])
                            tsl = slice(i * TT, (i + 1) * TT)
                            g = lambda j: GT[0:TT, i, j, h:h + 1]
                            egc = [g(2 + ch) for ch in range(nch)]
                            vns = [H["vn0"], H["vn1"]]
                            for ch in range(nch):
                                St = (Sst["p"][0] if kind == "p" else Sst["s"][i])[:, h, :]
                                S16t = (S16["p"][0] if kind == "p" else S16["s"][i])[:, h, :]
                                p5 = PB[4]
                                K.mm(p5[0:TT, 0:128], H["wT"][:, 0:TT], S16t)
                                K.mm(p5[0:TT, 128:256], qk16[h % 2][0][:, tsl], S16t)
                                yield
                                K.tt(vns[ch][0:TT], H["u"][0:TT], p5[0:TT, 0:128], ALU.subtract)
                                if ch == 0:
                                    K.ts(H["oacc"][0:TT], p5[0:TT, 128:256], egc[0], None, op0=mults)
                                else:
                                    K.stt(H["oacc"][0:TT], p5[0:TT, 128:256], egc[ch], H["oacc"][0:TT], mults, ALU.add)
                                yield
                                K.mm(p5[:, 256:384], H["kd2"][0:TT, ch, :], vns[ch][0:TT])
                                yield
                                K.stt(S16t, St, geB[:, i, ch, h:h + 1], p5[:, 256:384], mults, ALU.add)
                                K.stt(St, St, geB[:, i, ch, h:h + 1], p5[:, 256:384], mults, ALU.add)
                                yield
                            p6 = PB[4][:, 384:512]
                            for ch in range(nch):
                                K.mm(p6[0:TT, 0:128], H["qk2"][0:TT, ch, 0:TT], vns[ch][0:TT], start=(ch == 0), stop=(ch == nch - 1))
                            yield
                            K.tt(H["oacc"][0:TT], H["oacc"][0:TT], p6[0:TT, 0:128], ALU.add)
                            yield
                            ssA, sdA, rsA = H["c0"], H["c1"], H["c2"]
                            K.act(junkA[0:TT], H["oacc"][0:TT], AF.Square, accum=ssA[0:TT])
                            yield
                            K.act(sdA[0:TT], ssA[0:TT], AF.Ln, scale=1.0 / 128, bias=EPS)
                            yield
                            K.act(rsA[0:TT], sdA[0:TT], AF.Exp, scale=-0.5)
                            yield
                            K.ts(H["onb"][0:TT], H["oacc"][0:TT], rsA[0:TT], None, op0=mults)
                            yield
                            K.tr(TRB[:, 0:TT], H["onb"][0:TT, :], identB[0:TT, 0:TT])
                            yield
                            K.stt(ogT[:, h, tsl], TRB[:, 0:TT], ong[:, 0:1], zg[:, tsl], mults, mults)
                            yield

                    def proj_gen(h):
                        qkv = qkvb[h % 2]
                        zg = zgb[h % 2]
                        w = load_w([wcols(O_QKV + h * 128, 128), wcols(O_QKV + 1024 + h * 128, 128),
                                    wcols(O_QKV + 2048 + h * 128, 128), wcols(O_ZA + h * 128, 128)])
                        for j in range(3):
                            pj = PB[5]
                            proj_fm(pj, w, j * 128, 128, T)
                            ct = j * 8 + h
                            K.cp(xc[:, :, 0:3], hista[kind][:, ct, :, :], eng="pool")
                            yield
                            K.cp(xc[:, :, 3:3 + tl], pj[:, 0:T].v(RE("p (s t) -> p s t", s=nseq)), eng="act")
                            yield
                            K.cp(hista[kind][:, ct, :, :], xc[:, :, tl:tl + 3], eng="pool")
                            K.ts(acc, xc[:, :, 0:tl], caw[:, ct, 0:1], None, op0=mults)
                            yield
                            for k in range(1, 4):
                                K.stt(acc, xc[:, :, k:k + tl], caw[:, ct, k:k + 1], acc, mults, ALU.add)
                                yield
                            accf = acc.v(RE("p s t -> p (s t)"))
                            if j < 2:
                                K.act(sl, accf, AF.Silu)
                                yield
                                K.act(sq, sl, AF.Square)
                                yield
                                pss = PB[6]
                                K.mm(pss[:, 0:T], onesB, sq)
                                yield
                                K.act(sdd, pss[:, 0:T], AF.Ln, bias=EPS)
                                yield
                                K.act(sdd, sdd, AF.Exp, scale=-0.5)
                                yield
                                K.stt(qkv[j], sl, float(128 ** -0.5) if j == 0 else 1.0, sdd, mults, mults)
                                yield
                                K.cp(qk16[h % 2][j], qkv[j], eng="pool")
                                yield
                            else:
                                K.act(qkv[2], accf, AF.Silu)
                                yield
                        pj = PB[6]
                        proj_fm(pj, w, 384, 128, T)
                        yield
                        K.act(zg, pj[:, 0:T], AF.Silu)
                        yield

                    def head_pre(h):
                        return chain(proj_gen(h), lockgen([pre_gen(h, i) for i in range(NT)]))

                    lock([head_pre(0)])
                    for h in range(8):
                        gens = [scan_gen(h)]
                        if h < 7:
                            gens.append(head_pre(h + 1))
                        lock(gens)
                    K.soft()

            def phase_B():
                with contextlib.ExitStack() as eb_:
                    mk = lambda n, shp, dt=F32: K.sb(n, shp, dt, eb_)
                    cb = mk("cb", [128, 8, T]); ub = mk("ub", [128, nseq, 30 + tl]); sg = mk("sg", [128, T])
                    sqb = mk("sqb", [128, T]); ub16 = mk("ub16", [128, nseq, 30 + tl], BF16)

                    mean = mk("mean", [128, T]); rstdb = mk("rstdb", [128, T]); nmr = mk("nmr", [128, T])
                    tmp = mk("tmpb", [128, T]); tmp2 = mk("tmpb2", [128, T])
                    ubs = [(ub, ub16, sg), (mk("ub_b", [128, nseq, 30 + tl]), mk("ub16_b", [128, nseq, 30 + tl], BF16), mk("sg_b", [128, T]))]
                    for ct in range(8):
                        ub, ub16, sg = ubs[ct % 2]
                        w = load_w([wcols(O_GLU + ct * 128, 128), wcols(O_GLU + 1024 + ct * 128, 128), wcols(O_ZB + ct * 128, 128)])
                        pa = nextpb(0, 5); proj_fm(pa, w, 0, 128, T)
                        pb_ = nextpb(0, 5); proj_fm(pb_, w, 128, 128, T)
                        K.act(sg, pb_[:, 0:T], AF.Sigmoid)
                        K.cp(ub[:, :, 0:30], histb[kind][:, ct, :, :], eng="pool")
                        K.tt(ub[:, :, 30:30 + tl], pa[:, 0:T].v(RE("p (s t) -> p s t", s=nseq)), sg.v(RE("p (s t) -> p s t", s=nseq)), mults)
                        K.cp(histb[kind][:, ct, :, :], ub[:, :, tl:tl + 30], eng="pool")
                        K.cp(ub16, ub, eng="pool")
                        dgw = load_w([("dg", l, ct, 3968)])
                        dg = dgw.v(lambda a: a.rearrange("p k n -> p (k n)"))[:, 0:3968].v(lambda a: a.rearrange("p (k m) -> p k m", k=31))
                        pc = nextpb(0, 5)
                        pc3 = pc[:, 0:T].v(RE("p (s t) -> p s t", s=nseq))
                        for k in range(31):
                            K.mm(pc3, dg[:, k, :], ub16[:, :, k:k + tl], start=(k == 0), stop=(k == 30))
                        K.act(cb[:, ct, :], pc[:, 0:T], AF.Identity, bias=bv[:, 0, ct:ct + 1])
                        K.act(sqb, cb[:, ct, :], AF.Square)
                        K.mm(PB[5][:, 0:T], onesF, cb[:, ct, :], start=(ct == 0), stop=(ct == 7))
                        K.mm(PB[6][:, 0:T], onesF, sqb, start=(ct == 0), stop=(ct == 7))
                        pz = nextpb(0, 5); proj_fm(pz, w, 256, 128, T)
                        K.act(ubT[:, ct, 0:T], pz[:, 0:T], AF.Silu)
                    K.act(mean, PB[5][:, 0:T], AF.Copy, scale=1.0 / D)
                    K.tt(tmp, mean, mean, mults)
                    K.stt(tmp2, PB[6][:, 0:T], 1.0 / D, tmp, mults, ALU.subtract)
                    K.act(tmp2, tmp2, AF.Sqrt, bias=EPS)
                    K.recip(rstdb, tmp2)
                    K.stt(nmr, mean, -1.0, rstdb, mults, mults)
                    tmps = [(tmp, tmp2), (mk("tmpb3", [128, T]), mk("tmpb4", [128, T]))]
                    for ct in range(8):
                        ta, tb_ = tmps[ct % 2]
                        K.stt(ta, cb[:, ct, :], bv[:, 1, ct:ct + 1], rstdb, mults, mults)
                        K.stt(tb_, nmr, bv[:, 1, ct:ct + 1], ta, mults, ALU.add)
                        K.act(ta, tb_, AF.Silu, bias=bv[:, 2, ct:ct + 1])
                        K.tt(ubT[:, ct, 0:T], ta, ubT[:, ct, 0:T], mults)
                    K.soft()

            def phase_C():
                with contextlib.ExitStack() as ec_:
                    mk = lambda n, shp, dt=F32: K.sb(n, shp, dt, ec_)
                    qT = mk("qT", [128, 8, T], BF16); qiT = mk("qiT", [128, 4, T], BF16)
                    R = c.rounds
                    ec2 = contextlib.ExitStack()
                    mk2 = lambda n, shp, dt=F32: K.sb(n, shp, dt, ec2)
                    xb = mk2("xb", [128, T], BF16); r1 = mk2("r1", [128, T]); r2 = mk2("r2", [128, T])
                    rfm = mk2("rfm", [128, 4, T])
                    K.dma(rfm, rfd[:, :, tok0:tok0 + T])
                    if kind == "s":
                        kst = mk2("kst", [128, 8, 256], BF16); kist = mk2("kist", [128, 8, 128], BF16)
                        for s_ in range(nseq):
                            K.dma(kst, ck[l, s_].v(RE("(b p) f -> p b f", p=128)), q="pool")
                            for b in range(8):
                                K.dma(vaug[:, s_ * 9 + b, :, 0:128],
                                      cv[l, s_, b * 128:(b + 1) * 128, :].v(RE("p (h d) -> p h d", h=2)), q="pool")
                            K.dma(kist[:, :, 0:64], cki[l, s_].v(RE("(b p) f -> p b f", p=128)), q="pool")
                            K.dma(kist[:, :, 64:128], cki[l, s_].v(RE("(b p) f -> p b f", p=128)), q="pool")
                            for b in range(8):
                                for h in range(2):
                                    K.tr(TRB[:, h * 128:(h + 1) * 128], kst[:, b, h * 128:(h + 1) * 128], identB)
                                K.tr(TRB[:, 256:384], kist[:, b, :], identB)
                                c0 = s_ * LKS + b * 128
                                K.cp(kT[:, :, c0:c0 + 128], TRB[:, 0:256].v(RE("p (h t) -> p h t", h=2)), eng="act")
                                K.cp(kiT2[:, c0:c0 + 128], TRB[:, 256:384])
                    for h in range(8):
                        if h % 4 == 0:
                            w = load_w([wcols(O_QC + h * 128, 512)])
                        pq = nextpb(); proj_fm(pq, w, (h % 4) * 128, 128, T)
                        K.cp(xb, pq[:, 0:T], eng="act")
                        pr = nextpb(); K.mm(pr[:, 0:T], p128T, xb)
                        K.tt(r1, pq[:, 0:T], rfm[:, 0, 0:T], mults)
                        K.tt(r2, pr[:, 0:T], rfm[:, 1, 0:T], mults)
                        K.tt(qT[:, h, :], r1, r2, ALU.add)
                    w = load_w([wcols(O_QI, 512)])
                    for j in range(4):
                        pq = nextpb(); proj_fm(pq, w, j * 128, 128, T)
                        K.cp(xb, pq[:, 0:T], eng="act")
                        pr = nextpb(); K.mm(pr[:, 0:T], p64T, xb)
                        K.tt(r1, pq[:, 0:T], rfm[:, 2, 0:T], mults)
                        K.tt(r2, pr[:, 0:T], rfm[:, 3, 0:T], mults)
                        K.tt(qiT[:, j, :], r1, r2, ALU.add)
                    for h in range(8):
                        if h % 4 == 0:
                            w = load_w([wcols(O_ZC + h * 128, 512)])
                        pz = nextpb(); proj_fm(pz, w, (h % 4) * 128, 128, T)
                        K.act(ocT[:, h, 0:T], pz[:, 0:T], AF.Silu)
                    ec2.close()
                    K.soft()
                    NKMAX = max(TP if kind == "p" else 0, LKS + 96)
                    NKMAX = ((NKMAX + 127) // 128) * 128
                    sc = mk("sc", [128, NKMAX]); cmpb = mk("cmpb", [128, NKMAX], BF16)
                    maskT = mk("maskT", [128, NKMAX // 128, 128], BF16)
                    rl = [mk("rl%d" % i_, [128, 512]) for i_ in range(2)]
                    pt = [mk("pt%d" % i_, [128, 4, 128], BF16) for i_ in range(3)]
                    ob = [mk("ob%d" % i_, [128, 128], BF16) for i_ in range(2)]
                    dtab = mk("dtab", [128, R + 2]); ndt = mk("ndt", [128, R + 2])
                    PV = [PB[4], PB[5], PB[6]]
                    Bc, d0, mid, cnt, uu, thr, rden = col[6], col[7], col[8], col[9], col[10], col[11], col[0]
                    maskTs = [maskT, mk("maskT2", [128, NKMAX // 128, 128], BF16)]

                    def tile_info(i):
                        if kind == "p":
                            qt = tok0 // 128 + i
                            blocks = [(b * 128, 128, b) for b in range(qt + 1)]
                            return blocks, (qt + 1) * 128, True
                        blocks = [(i * LKS + b * 128, 128, i * 9 + b) for b in range(8)] + [(i * LKS + PAST, TS, i * 9 + 8)]
                        return blocks, LKS, False

                    def X_gen(i):
                        qn = TT
                        qsl = slice(i * TT, (i + 1) * TT)
                        blocks, N, masked = tile_info(i)
                        mT_ = maskTs[i % 2]
                        col0 = blocks[0][0]
                        for sb0 in range(0, N, 512):
                            n = min(512, N - sb0)
                            for j in range(8):
                                pl = nextpb(0, 2)
                                hr = slice((j % 2) * 64, (j % 2) * 64 + 64)
                                K.mm(pl[0:qn, 0:n], qiT[hr, j // 2, qsl], kiT2[hr, col0 + sb0: col0 + sb0 + n])
                                r = rl[j % 2]
                                K.act(r[0:qn, 0:n], pl[0:qn, 0:n], AF.Relu)
                                if j == 0:
                                    K.ts(sc[0:qn, sb0:sb0 + n], r[0:qn, 0:n], wtok[0:qn, i, 0:1], None, op0=mults)
                                else:
                                    K.stt(sc[0:qn, sb0:sb0 + n], r[0:qn, 0:n], wtok[0:qn, i, j:j + 1], sc[0:qn, sb0:sb0 + n], mults, ALU.add)
                                yield
                        lastc = slice(N - 128, N)
                        if masked:
                            K.tt(sc[0:qn, lastc], sc[0:qn, lastc], adm01[0:qn, :], mults)
                        K.reduce(Bc[0:qn], sc[0:qn, 0:N], ALU.max, absval=True)
                        yield
                        K.ts(d0[0:qn], Bc[0:qn], 1.0, None, op0=ALU.add)
                        if masked:
                            K.tt(sc[0:qn, lastc], sc[0:qn, lastc], admneg[0:qn, :], ALU.add)
                        yield
                        for r_ in range(R + 1):
                            K.ts(dtab[0:qn, r_:r_ + 1], d0[0:qn], float(2.0 ** -r_), None, op0=mults, eng="pool")
                        K.memset(mid[0:qn], 0.0)
                        K.ts(ndt[0:qn, 0:R + 1], dtab[0:qn, 0:R + 1], -1.0, None, op0=mults, eng="pool")
                        yield
                        thrc = float(2 * nsel - N)
                        for r_ in range(R):
                            K.act(cmpb[0:qn, 0:N], sc[0:qn, 0:N], AF.Sign, bias=mid[0:qn], accum=cnt[0:qn])
                            yield
                            K.act(uu[0:qn], cnt[0:qn], AF.Sign, bias=float(0.5 - thrc))
                            yield
                            K.act(mid[0:qn], uu[0:qn], AF.Identity, scale=ndt[0:qn, r_ + 1:r_ + 2], bias=mid[0:qn])
                            yield
                        K.tt(thr[0:qn], mid[0:qn], dtab[0:qn, R:R + 1], ALU.add)
                        yield
                        K.ts(cmpb[0:qn, 0:N], sc[0:qn, 0:N], thr[0:qn], 0.0, op0=ALU.add, op1=ALU.is_ge)
                        yield
                        nfull = sum(1 for b in blocks if b[1] == 128)
                        for g0 in range(0, nfull, 8):
                            m = min(8, nfull - g0)
                            for bb in range(m):
                                K.tr(TRB[:, bb * 128: bb * 128 + qn], cmpb[0:qn, (g0 + bb) * 128:(g0 + bb + 1) * 128], identB[0:qn, 0:qn])
                            K.cp(mT_[:, g0:g0 + m, 0:qn], TRB.v(RE("p (b t) -> p b t", b=8))[:, 0:m, 0:qn],
                                 eng=("act" if (g0 // 8) % 2 else "dve"))
                            yield
                        for bi, (kc0, kn, vblk) in enumerate(blocks):
                            if kn != 128:
                                K.tr(TRB[0:kn, 0:qn], cmpb[0:qn, bi * 128: bi * 128 + kn], identB[0:qn, 0:qn])
                                K.cp(mT_[0:kn, bi, 0:qn], TRB[0:kn, 0:qn])
                                yield

                    def Y_gen(i):
                        qn = TT
                        qsl = slice(i * TT, (i + 1) * TT)
                        blocks, N, masked = tile_info(i)
                        mT_ = maskTs[i % 2]
                        first = [True, True, True]
                        nblk = len(blocks)
                        steps = [(bi, g) for bi in range(nblk) for g in range(2)]

                        def stA(k):
                            bi, g = steps[k]
                            kc0, kn, vblk = blocks[bi]
                            st = nextpb(2, 4)
                            st3 = st[0:kn, 0:4 * qn].v(RE("p (h t) -> p h t", h=4))
                            K.mm(st3, kT[:, g, kc0:kc0 + kn], qT[:, 4 * g:4 * g + 4, qsl])
                            p = pt[k % 3]
                            K.act(p[0:kn, :, 0:qn], st3, AF.Exp)
                            K.tt(p[0:kn, :, 0:qn], p[0:kn, :, 0:qn],
                                 mT_[0:kn, bi, 0:qn].v(lambda a: a.unsqueeze(1).broadcast_to([kn, 4, qn])), mults,
                                 eng=("pool" if k % 2 else "dve"))

                        def stB(k):
                            bi, g = steps[k]
                            kc0, kn, vblk = blocks[bi]
                            p = pt[k % 3]
                            for hh in range(4):
                                h = 4 * g + hh
                                bk = h // 3
                                off = (h % 3) * 129
                                K.mm(PV[bk][0:qn, off:off + 129], p[0:kn, hh, 0:qn], vaug[0:kn, vblk, g, 0:129],
                                     start=first[bk], stop=(bi == nblk - 1), skip_group_check=True)
                                first[bk] = False

                        for k in range(len(steps)):
                            stA(k)
                            if k > 0:
                                stB(k - 1)
                            yield
                        stB(len(steps) - 1)
                        yield
                        for h in range(8):
                            bk = h // 3
                            off = (h % 3) * 129
                            o_ = ob[h % 2]
                            K.recip(rden[0:qn], PV[bk][0:qn, off + 128:off + 129])
                            K.ts(o_[0:qn], PV[bk][0:qn, off:off + 128], rden[0:qn], None, op0=mults)
                            K.tr(TRB[:, h * 128:h * 128 + qn], o_[0:qn, :], identB[0:qn, 0:qn])
                            K.tt(ocT[:, h, qsl], TRB[:, h * 128:h * 128 + qn], ocT[:, h, qsl], mults)
                            yield

                    lock([X_gen(0)])
                    for i in range(NT):
                        gens = [Y_gen(i)]
                        if i + 1 < NT:
                            gens.append(X_gen(i + 1))
                        lock(gens)
                    K.soft()

            def phase_M():
                mrange = (5, 7) if overlap_next else (0, 7)

                def gen(em_):
                    mk = lambda n, shp, dt=F32: K.sb(n, shp, dt, em_)
                    sga = mk("sga", [128, T]); macc = mk("macc", [128, T]); tmp = mk("tmpm", [128, T])
                    xt = mk("xtm", [128, D]); xo = mk("xo", [128, D]); junk = mk("junkm", [128, D], BF16)
                    mT = mk("mT", [128, 8, SB], BF16)
                    if last:
                        fngB = mk("fngB", [128, D])
                        K.dma(fngB, fng)
                    for n in range(8):
                        wy = load_w([("oa", l, n * 128, 128), ("pb", l, n * 128, 128), ("oc", l, n * 128, 128)])
                        wg = load_w([wcols(O_GATE + n * 128, 128), wcols(O_GATE + 1024 + n * 128, 128), wcols(O_GATE + 2048 + n * 128, 128)])
                        for bi, src in enumerate((ogT, ubT, ocT)):
                            py = nextpb(*mrange); proj_fm(py, wy, bi * 128, 128, T, rhs=src)
                            pg = nextpb(*mrange); proj_fm(pg, wg, bi * 128, 128, T)
                            K.act(sga, pg[:, 0:T], AF.Sigmoid)
                            if bi == 0:
                                K.tt(macc, py[:, 0:T], sga, mults)
                            else:
                                K.tt(tmp, py[:, 0:T], sga, mults)
                                K.tt(macc, macc, tmp, ALU.add)
                            yield
                        K.cp(mT[:, n, 0:T], macc, eng="act")
                        yield
                    wo = [load_w([("out", l, hf * 512, 512)]) for hf in range(2)]
                    ss, sd, rstd = col[0], col[1], col[2]
                    for i in range(NT):
                        r0, r1_ = tok0 + i * TT, tok0 + (i + 1) * TT
                        K.dma(xt[0:TT], xin[r0:r1_, :])
                        for hf in range(2):
                            po = nextpb(*mrange)
                            for k in range(8):
                                K.mm(po[0:TT, 0:512], mT[:, k, i * TT:(i + 1) * TT], wo[hf][:, k, :], start=(k == 0), stop=(k == 7))
                            K.tt(xo[0:TT, hf * 512:(hf + 1) * 512], po[0:TT, 0:512], xt[0:TT, hf * 512:(hf + 1) * 512], ALU.add)
                        if not last:
                            K.dma(xout[r0:r1_, :], xo[0:TT])
                        else:
                            K.act(junk[0:TT], xo[0:TT], AF.Square, accum=ss[0:TT])
                            K.act(sd[0:TT], ss[0:TT], AF.Sqrt, scale=1.0 / D, bias=EPS)
                            K.recip(rstd[0:TT], sd[0:TT])
                            K.stt(xt[0:TT], xo[0:TT], rstd[0:TT], fngB[0:TT], mults, mults)
                            K.dma(yout[r0:r1_, :], xt[0:TT])
                        yield

                def run():
                    with contextlib.ExitStack() as em_:
                        gens = [gen(em_)]
                        if overlap_next:
                            gens += make_prologue(em_, tok0 + SB, hTs[(sbi + 1) % 2])
                        lock(gens)
                        K.soft()
                run()

            phase_A()
            if c.stages < 3:
                return
            with contextlib.ExitStack() as e2:
                ubT = K.sb("ubT", [128, 8, SB], BF16, e2)
                phase_B()
                if c.stages < 4:
                    return
                ocT = K.sb("ocT", [128, 8, SB], BF16, e2)
                phase_C()
                if c.stages < 5:
                    return
                phase_M()

        convert_layer(0)
        for l in range(L):
            if l + 1 < L:
                convert_layer(l + 1)
            K.dma(gB, normg[l]); K.dma(caw, convaw[l]); K.dma(hp, hp8[l]); K.dma(ong, onormg[l])
            K.dma(cbw, convbw[l]); K.dma(bv, bvec[l])
            K.act(ea, hp[:, 0:8], AF.Exp)
            with contextlib.ExitStack() as esd:
                stg = [K.sb("stg%d" % i_, [128, 31, 128], BF16, esd) for i_ in range(2)]
                for ct_ in range(8):
                    for k_ in range(31):
                        K.act(stg[ct_ % 2][:, k_, :], identB, AF.Copy, scale=cbw[:, ct_, k_:k_ + 1])
                    K.dma(dgd[l][ct_], stg[ct_ % 2].v(lambda a: a.rearrange("p k m -> p (k m)")), par=True)
                K.soft()
            K.memset(Sst["p"][0], 0.0); K.memset(hista["p"], 0.0); K.memset(histb["p"], 0.0)
            K.memset(S16["p"][0], 0.0)
            nsb = TP // SB
            for sbi in range(nsb):
                run_sb(l, "p", sbi, pro_done=(sbi > 0), overlap_next=(sbi + 1 < nsb))
            if c.stages >= 2:
                K.dma(cap[l], hista["p"][:, :, 0, :])
                K.dma(dlp[l].v(lambda a: a.rearrange("h d e -> d h e")), Sst["p"][0])
            if c.stages >= 3:
                K.dma(cbp[l], histb["p"][:, :, 0, :])
            with contextlib.ExitStack() as ess:
                Sst["s"] = [K.sb("S_s%d" % i, [128, 8, 128], F32, ess) for i in range(NSS)]
                S16["s"] = [K.sb("S16_s%d" % i, [128, 8, 128], BF16, ess) for i in range(NSS)]
                hista["s"] = K.sb("hista_s", [128, 24, NSS, 3], F32, ess)
                histb["s"] = K.sb("histb_s", [128, 8, NSS, 30], F32, ess)
                for s_ in range(NSS):
                    K.dma(hista["s"][:, :, s_, :], sca[l, s_])
                    K.dma(Sst["s"][s_], sdl[l, s_].v(lambda a: a.rearrange("h d e -> d h e")))
                    K.cp(S16["s"][s_], Sst["s"][s_], eng="pool")
                    K.dma(histb["s"][:, :, s_, :], scb[l, s_])
                run_sb(l, "s", 0)
                for s_ in range(NSS):
                    if c.stages >= 2:
                        K.dma(cas[l, s_], hista["s"][:, :, s_, :])
                        K.dma(dls[l, s_].v(lambda a: a.rearrange("h d e -> d h e")), Sst["s"][s_])
                    if c.stages >= 3:
                        K.dma(cbs[l, s_], histb["s"][:, :, s_, :])
                K.soft()
        K.barrier()
        K.finish()
    return nc, K, rec


def _prep(inp, cfg):
    c = cfg
    L, TP, NSS, TS, PAST = c.depth, c.tp, c.nss, c.ts, c.past
    f = lambda a: np.ascontiguousarray(np.asarray(a, dtype=np.float32))
    shared = {
        "w_in": f(inp["w_in"]), "w_oa": f(inp["w_o_a"]), "w_pb": f(inp["w_pw2_b"]), "w_oc": f(inp["w_o_c"]),
        "w_out": f(inp["w_out"]),
        "normg": f(np.broadcast_to(np.asarray(inp["norm_g"])[:, None, :], (L, 128, D))),
        "fng": f(np.broadcast_to(np.asarray(inp["final_norm_g"])[None, :], (128, D))),
        "convaw": f(np.asarray(inp["conv_a_w"]).reshape(L, 4, 24, 128).transpose(0, 3, 2, 1)),
        "hp8": f(np.broadcast_to(np.concatenate([np.asarray(inp["a_log"]), np.asarray(inp["dt_bias"])], 1)[:, None, :], (L, 128, 16))),
        "onormg": f(np.asarray(inp["onorm_a_g"]).reshape(L, 128, 1)),
        "convbw": f(np.asarray(inp["conv_b_w"]).reshape(L, 31, 8, 128).transpose(0, 3, 2, 1)),
        "bvec": f(np.stack([np.asarray(inp[k]).reshape(L, 8, 128).transpose(0, 2, 1) for k in ("conv_b_bias", "ln_b_g", "ln_b_b")], 2)),
        "cmisc": misc_consts(), "cchp": chunk_consts(128, 64), "cchs": chunk_consts(TS, TS),
    }
    fm, tm = rope_tables(np.arange(TP)); shared["rfm_p"], shared["rtm_p"] = fm, tm
    fm, tm = rope_tables(PAST + np.arange(TS))
    shared["rfm_s"] = np.ascontiguousarray(np.tile(fm, (1, 1, NSS))); shared["rtm_s"] = np.ascontiguousarray(np.tile(tm, (NSS, 1)))
    maps = []
    ncore = 8
    xpr = np.asarray(inp["x_prompt"]); xsm = np.asarray(inp["x_sample"])
    for core in range(ncore):
        m = dict(shared)
        m["xp"] = f(xpr[core // 2][:TP])
        sl = slice(core * NSS, (core + 1) * NSS)
        m["xs"] = f(xsm[sl].reshape(NSS * TS, D))
        m["ck"] = f(np.asarray(inp["cache_k"])[:, sl].reshape(L, NSS, PAST, 256))
        m["cv"] = f(np.asarray(inp["cache_v"])[:, sl].reshape(L, NSS, PAST, 256))
        m["cki"] = f(np.asarray(inp["cache_idx_k"])[:, sl])
        m["sca"] = f(np.asarray(inp["state_conv_a"])[:, sl].reshape(L, NSS, 3, 24, 128).transpose(0, 1, 4, 3, 2))
        m["sdl"] = f(np.asarray(inp["state_delta"])[:, sl])
        m["scb"] = f(np.asarray(inp["state_conv_b"])[:, sl].reshape(L, NSS, 30, 8, 128).transpose(0, 1, 4, 3, 2))
        maps.append(m)
    return maps


def _post(res, cfg):
    c = cfg
    L, TP, NSS, TS = c.depth, c.tp, c.nss, c.ts
    R = res.results
    ev = [R[i] for i in range(0, 8, 2)]
    cat = lambda k, rs: np.stack([r[k] for r in rs], 0)
    y_p = cat("yp", ev)
    y_s = np.concatenate([r["ys"].reshape(NSS, TS, D) for r in R], 0)
    nk_p = cat("nkp", ev).transpose(1, 0, 2, 3).reshape(L, 4, TP, 2, 128)
    nv_p = cat("nvp", ev).transpose(1, 0, 2, 3).reshape(L, 4, TP, 2, 128)
    nki_p = cat("nkip", ev).transpose(1, 0, 2, 3)
    ca_p = cat("cap", ev).transpose(1, 0, 4, 3, 2).reshape(L, 4, 3, 3072)
    dl_p = cat("dlp", ev).transpose(1, 0, 2, 3, 4)
    cb_p = cat("cbp", ev).transpose(1, 0, 4, 3, 2).reshape(L, 4, 30, 1024)
    nk_s = np.concatenate([r["nks"].reshape(L, NSS, TS, 2, 128) for r in R], 1)
    nv_s = np.concatenate([r["nvs"].reshape(L, NSS, TS, 2, 128) for r in R], 1)
    nki_s = np.concatenate([r["nkis"].reshape(L, NSS, TS, 64) for r in R], 1)
    ca_s = np.concatenate([r["cas"].transpose(0, 1, 4, 3, 2).reshape(L, NSS, 3, 3072) for r in R], 1)
    dl_s = np.concatenate([r["dls"] for r in R], 1)
    cb_s = np.concatenate([r["cbs"].transpose(0, 1, 4, 3, 2).reshape(L, NSS, 30, 1024) for r in R], 1)
    outs = (y_p, y_s, nk_p, nv_p, nki_p, ca_p, dl_p, cb_p, nk_s, nv_s, nki_s, ca_s, dl_s, cb_s)
    return tuple(np.ascontiguousarray(o, dtype=np.float32) for o in outs)


CFG = Cfg()


def kernel(**inputs):
    _, _, sched = build(CFG)
    nc, _, _ = build(CFG, sched)
    maps = _prep(inputs, CFG)
    res = run_bass_kernel_spmd(nc, maps, core_ids=list(range(8)))
    return _post(res, CFG)
```
